# Optimizing a Trainium2 kernel written in Bass

```python
import jax, jax.numpy as jnp
from jax import lax
import numpy as np

D_MODEL = 2048
BATCH = 8
SEQ = 2048
DEPTH = 4

N_MIXERS = 4
EPS = 1e-6
CHUNK = 64
RET_HEADS = 8
RET_DK = D_MODEL // RET_HEADS
RET_DV = 2 * RET_DK
RET_IN = 2 * RET_HEADS * RET_DK + 2 * RET_HEADS * RET_DV
ROPE_BASE = 10000.0
CONV_WIDTH = 3
GLA_HEADS = 4
GLA_DK = D_MODEL // 2 // GLA_HEADS
GLA_DV = D_MODEL // GLA_HEADS
GLA_GATE_RANK = 16
GLA_GATE_TAU = 16.0
GLA_IN = 2 * GLA_HEADS * GLA_DK + 2 * GLA_HEADS * GLA_DV + GLA_GATE_RANK
POOL_WINDOWS = (2, 4, 8, 16)
POOL_GROUPS = len(POOL_WINDOWS)
POOL_GROUP = D_MODEL // POOL_GROUPS
D_FF = -(-8 * D_MODEL // (3 * 256)) * 256
N_RET = (DEPTH + 3) // N_MIXERS
N_CONV = (DEPTH + 2) // N_MIXERS
N_GLA = (DEPTH + 1) // N_MIXERS
N_POOL = DEPTH // N_MIXERS

kernel_name = "interleaved_hybrid_ret_conv_gla_pool_adaln"


def rms_norm(x, gain=None):
    xf = x.astype(jnp.float32)
    y = xf * lax.rsqrt(jnp.mean(xf * xf, axis=-1, keepdims=True) + EPS)
    if gain is not None:
        y = y * gain.astype(jnp.float32)
    return y


def rope(t, cos, sin):
    t = t.astype(jnp.float32)
    half = t.shape[-1] // 2
    t1, t2 = t[..., :half], t[..., half:]
    return jnp.concatenate([t1 * cos - t2 * sin, t2 * cos + t1 * sin], axis=-1)


def chunked_gated_linear_attention(q, k, v, log_a):
    B, H, S, dk = q.shape
    dv = v.shape[-1]
    n = S // CHUNK

    def to_chunks(t):
        t = t.astype(jnp.float32)
        return t.reshape(t.shape[0], t.shape[1], n, CHUNK, t.shape[-1]).transpose(2, 0, 1, 3, 4)

    qc, kc, vc, gc = to_chunks(q), to_chunks(k), to_chunks(v), to_chunks(log_a)
    causal = jnp.tril(jnp.ones((CHUNK, CHUNK), dtype=bool))

    def step(state, inp):
        qi, ki, vi, gi = inp
        b = jnp.cumsum(gi, axis=-2)
        b_last = b[..., -1:, :]
        q_dec = qi * jnp.exp(b)
        k_dec = ki * jnp.exp(-b)
        scores = jnp.where(causal, jnp.einsum('bhid,bhjd->bhij', q_dec, k_dec), 0.0)
        o = jnp.einsum('bhij,bhjv->bhiv', scores, vi) + jnp.einsum('bhid,bhdv->bhiv', q_dec, state)
        k_carry = ki * jnp.exp(b_last - b)
        state = jnp.exp(b_last)[..., 0, :, None] * state + jnp.einsum('bhjd,bhjv->bhdv', k_carry, vi)
        return state, o

    state0 = jnp.zeros((B, H, dk, dv), jnp.float32)
    _, o = lax.scan(step, state0, (qc, kc, vc, gc))
    return o.transpose(1, 2, 0, 3, 4).reshape(B, H, S, dv)


def split_heads(t, n_heads):
    B, S, W = t.shape
    return t.reshape(B, S, n_heads, W // n_heads).transpose(0, 2, 1, 3)


def merge_heads(t):
    B, H, S, d = t.shape
    return t.transpose(0, 2, 1, 3).reshape(B, S, H * d)


def retention_mixer(h, cos, sin, w_in, w_out):
    B, S, _ = h.shape
    qk_w, v_w = RET_HEADS * RET_DK, RET_HEADS * RET_DV
    q, k, v, g = jnp.split(h @ w_in, [qk_w, 2 * qk_w, 2 * qk_w + v_w], axis=-1)
    q = rope(split_heads(q, RET_HEADS), cos, sin)
    k = rope(split_heads(k, RET_HEADS), cos, sin) * (RET_DK ** -0.5)
    v = split_heads(v, RET_HEADS)
    log_gamma = jnp.log1p(-jnp.exp2(-5.0 - jnp.arange(RET_HEADS, dtype=jnp.float32)))
    log_a = jnp.broadcast_to(log_gamma[None, :, None, None], (1, RET_HEADS, S, 1))
    o = rms_norm(chunked_gated_linear_attention(q, k, v, log_a))
    o = merge_heads(o).astype(h.dtype)
    return ((jax.nn.silu(g) * o) @ w_out).astype(h.dtype)


def short_conv_mixer(h, w_in, conv_w, w_out):
    S = h.shape[1]
    b_gate, c_gate, u = jnp.split(h @ w_in, 3, axis=-1)
    u = c_gate * u
    up = jnp.pad(u, ((0, 0), (CONV_WIDTH - 1, 0), (0, 0)))
    y = sum(conv_w[j] * up[:, j:j + S] for j in range(CONV_WIDTH))
    return ((b_gate * y) @ w_out).astype(h.dtype)


def gla_mixer(h, w_in, w_gate_up, b_gate, w_out):
    qk_w, v_w = GLA_HEADS * GLA_DK, GLA_HEADS * GLA_DV
    q, k, v, g, z = jnp.split(h @ w_in, [qk_w, 2 * qk_w, 2 * qk_w + v_w, 2 * qk_w + 2 * v_w], axis=-1)
    log_a = jax.nn.log_sigmoid((z @ w_gate_up + b_gate).astype(jnp.float32)) / GLA_GATE_TAU
    q = split_heads(q, GLA_HEADS).astype(jnp.float32) * (GLA_DK ** -0.5)
    o = chunked_gated_linear_attention(q, split_heads(k, GLA_HEADS), split_heads(v, GLA_HEADS),
                                       split_heads(log_a, GLA_HEADS))
    o = merge_heads(rms_norm(o)).astype(h.dtype)
    return ((jax.nn.silu(g) * o) @ w_out).astype(h.dtype)


def pool_mixer(h, w_group, scale):
    B, S, D = h.shape
    hf = h.astype(jnp.float32).reshape(B, S, POOL_GROUPS, POOL_GROUP)
    cs = jnp.concatenate([jnp.zeros((B, 1, POOL_GROUPS, POOL_GROUP), jnp.float32),
                          jnp.cumsum(hf, axis=1)], axis=1)
    t = jnp.arange(S)
    pooled = []
    for gi, win in enumerate(POOL_WINDOWS):
        start = jnp.maximum(t + 1 - win, 0)
        total = cs[:, 1:, gi] - cs[:, start, gi]
        count = (t + 1 - start).astype(jnp.float32)
        pooled.append(total / count[None, :, None])
    mixed = jnp.stack(pooled, axis=2) - hf
    y = jnp.einsum('bsgp,gpq->bsgq', mixed, w_group.astype(jnp.float32)).reshape(B, S, D)
    return (y * scale).astype(h.dtype)


def swiglu(h, w_in, w_out):
    a, b = jnp.split(h @ w_in, 2, axis=-1)
    return (jax.nn.silu(a) * b) @ w_out


def _normal(key, shape, scale):
    return jax.random.normal(key, shape, jnp.float32) * scale


def setup_inputs(seed: int = 0) -> dict:
    key = jax.random.key(seed)
    ks = jax.random.split(key, 20)
    D = D_MODEL
    return {
        "x": _normal(ks[0], (BATCH, SEQ, D), 1.0),
        "c": _normal(ks[1], (BATCH, D), 1.0),
        "positions": jnp.broadcast_to(jnp.arange(SEQ, dtype=jnp.int32), (BATCH, SEQ)),
        "w_mod": _normal(ks[2], (DEPTH, D, 6 * D), 0.5 * D ** -0.5),
        "b_mod": _normal(ks[3], (DEPTH, 6 * D), 0.02),
        "norm1_g": 1.0 + _normal(ks[4], (DEPTH, D), 0.02),
        "norm2_g": 1.0 + _normal(ks[5], (DEPTH, D), 0.02),
        "ret_w_in": _normal(ks[6], (N_RET, D, RET_IN), D ** -0.5),
        "ret_w_out": _normal(ks[7], (N_RET, RET_HEADS * RET_DV, D), (RET_HEADS * RET_DV) ** -0.5),
        "conv_w_in": _normal(ks[8], (N_CONV, D, 3 * D), D ** -0.5),
        "conv_w": _normal(ks[9], (N_CONV, CONV_WIDTH, D), CONV_WIDTH ** -0.5),
        "conv_w_out": _normal(ks[10], (N_CONV, D, D), D ** -0.5),
        "gla_w_in": _normal(ks[11], (N_GLA, D, GLA_IN), D ** -0.5),
        "gla_w_gate_up": _normal(ks[12], (N_GLA, GLA_GATE_RANK, GLA_HEADS * GLA_DK), GLA_GATE_RANK ** -0.5),
        "gla_b_gate": _normal(ks[13], (N_GLA, GLA_HEADS * GLA_DK), 0.02),
        "gla_w_out": _normal(ks[14], (N_GLA, GLA_HEADS * GLA_DV, D), (GLA_HEADS * GLA_DV) ** -0.5),
        "pool_w": _normal(ks[15], (N_POOL, POOL_GROUPS, POOL_GROUP, POOL_GROUP), POOL_GROUP ** -0.5),
        "pool_scale": 1.0 + _normal(ks[16], (N_POOL, D), 0.1),
        "ffn_w_in": _normal(ks[17], (DEPTH, D, 2 * D_FF), D ** -0.5),
        "ffn_w_out": _normal(ks[18], (DEPTH, D_FF, D), D_FF ** -0.5),
        "final_g": 1.0 + _normal(ks[19], (D,), 0.02),
    }


def reference(x, c, positions, w_mod, b_mod, norm1_g, norm2_g, ret_w_in, ret_w_out,
              conv_w_in, conv_w, conv_w_out, gla_w_in, gla_w_gate_up, gla_b_gate, gla_w_out,
              pool_w, pool_scale, ffn_w_in, ffn_w_out, final_g):
    half = RET_DK // 2
    inv_freq = jnp.power(ROPE_BASE, -jnp.linspace(0.0, 1.0, half, dtype=jnp.float32))
    ang = positions.astype(jnp.float32)[:, None, :, None] * inv_freq
    cos, sin = jnp.cos(ang), jnp.sin(ang)
    c_act = jax.nn.silu(c)
    for i in range(DEPTH):
        mod = (c_act @ w_mod[i] + b_mod[i])[:, None, :]
        sh1, sc1, g1, sh2, sc2, g2 = jnp.split(mod, 6, axis=-1)
        h = (rms_norm(x, norm1_g[i]) * (1.0 + sc1) + sh1).astype(x.dtype)
        m, j = i % N_MIXERS, i // N_MIXERS
        if m == 0:
            y = retention_mixer(h, cos, sin, ret_w_in[j], ret_w_out[j])
        elif m == 1:
            y = short_conv_mixer(h, conv_w_in[j], conv_w[j], conv_w_out[j])
        elif m == 2:
            y = gla_mixer(h, gla_w_in[j], gla_w_gate_up[j], gla_b_gate[j], gla_w_out[j])
        else:
            y = pool_mixer(h, pool_w[j], pool_scale[j])
        x = (x + g1 * y).astype(x.dtype)
        h = (rms_norm(x, norm2_g[i]) * (1.0 + sc2) + sh2).astype(x.dtype)
        x = (x + g2 * swiglu(h, ffn_w_in[i], ffn_w_out[i])).astype(x.dtype)
    return rms_norm(x, final_g).astype(x.dtype)
```

```python
import math
import os
from contextlib import ExitStack

import numpy as np
import concourse.bass as bass
import concourse.mybir as mybir
from concourse.bass_utils import run_bass_kernel_spmd

F32 = mybir.dt.float32
BF16 = mybir.dt.bfloat16
I32 = mybir.dt.int32
AF = mybir.ActivationFunctionType
ALU = mybir.AluOpType
ENGS = ["sync", "scalar", "vector", "gpsimd", "tensor"]
SEM_LIMIT = 16000
SAME_ENGINE_SYNC = {"scalar": True, "vector": True, "gpsimd": True, "tensor": False, "sync": False}

D = 2048
S = 2048
L = 4
KC = 16
DFF = 5632
JC = 44
EPS = 1e-6
PI = math.pi
NSLOT = 3
SLOT = 8192


class Sem:
    __slots__ = ("h", "n", "name", "eng")

    def __init__(self, h, name, eng=None):
        self.h, self.n, self.name, self.eng = h, 0, name, eng


class Prog:
    def __init__(self, nc, stack):
        self.nc, self.stack = nc, stack
        self.q = {e: [] for e in ENGS}
        self.nsem = 0
        self.prog = {e: None for e in ENGS}
        self.last_w = {}
        self.readers = {}
        self.pending = {e: [] for e in ENGS}

    def sem(self, name, eng=None):
        self.nsem += 1
        return Sem(self.stack.enter_context(self.nc.semaphore(f"{name}_{self.nsem}")), name, eng)

    def sbuf(self, name, shape, dt):
        return self.stack.enter_context(self.nc.sbuf_tensor(name, list(shape), dt))

    def psum(self, name, shape, dt=F32):
        return self.stack.enter_context(self.nc.psum_tensor(name, list(shape), dt))

    def _deps(self, eng, reads, writes, extra):
        w = {}

        def add(tok):
            if tok is None:
                return
            s, v = tok
            if s.eng == eng and not SAME_ENGINE_SYNC[eng]:
                return
            if w.get(id(s), (s, 0))[1] < v:
                w[id(s)] = (s, v)

        for k in reads:
            add(self.last_w.get(k))
        for k in writes:
            add(self.last_w.get(k))
            for t in self.readers.get(k, {}).values():
                add(t)
        for t in extra:
            add(t)
        return list(w.values())

    def _register(self, tok, reads, writes):
        s, v = tok
        for k in reads:
            self.readers.setdefault(k, {})[id(s)] = tok
        for k in writes:
            self.last_w[k] = tok
            self.readers[k] = {}

    def op(self, eng, fn, reads=(), writes=(), mark=True, extra=()):
        waits = self._deps(eng, reads, writes, extra)
        tok = None
        sem = None
        if mark:
            sem = self.prog[eng]
            if sem is None or sem.n >= SEM_LIMIT:
                sem = self.prog[eng] = self.sem("pg_" + eng, eng)
            sem.n += 1
            tok = (sem, sem.n)
            for r, wr in self.pending[eng]:
                self._register(tok, r, wr)
            self.pending[eng] = []
            self._register(tok, reads, writes)
        else:
            self.pending[eng].append((tuple(reads), tuple(writes)))
        self.q[eng].append((fn, waits, sem, 1))
        return tok

    def dma(self, eng, out, in_, sem, reads=(), writes=(), extra=()):
        waits = self._deps(eng, reads, writes, extra)
        sem.n += 16
        tok = (sem, sem.n)
        self._register(tok, reads, writes)
        self.q[eng].append((lambda e: e.dma_start(out=out, in_=in_), waits, sem, 16))
        return tok

    def wait_all(self, eng, toks):
        self.q[eng].append((None, self._deps(eng, (), (), toks), None, 0))

    def run(self):
        for e in ENGS:
            assert not self.pending[e], f"unmarked trailing ops on {e}"
        with self.nc.Block() as block:
            for eng in ENGS:
                getattr(block, eng)(lambda e, eng=eng: self._replay(e, eng))

    def _replay(self, e, eng):
        waited = {}
        for fn, waits, sem, by in self.q[eng]:
            for s, v in waits:
                if waited.get(id(s), 0) >= v:
                    continue
                e.wait_ge(s.h, v)
                waited[id(s)] = v
            if fn is None:
                continue
            ins = fn(e)
            if sem is not None:
                ins.then_inc(sem.h, by)


class RR:
    def __init__(self, items):
        self.items, self.i = items, 0

    def next(self):
        it = self.items[self.i % len(self.items)]
        self.i += 1
        return it


def build_program(nsub=2 * L, final=True):
    nc = bass.Bass("TRN2", target_bir_lowering=False)

    def din(name, shape, dt=F32):
        return nc.dram_tensor(name, list(shape), dt, kind="ExternalInput").ap()

    xT_d = din("xT", [D, S])
    c_d = din("c_l", [128, KC])
    pos_d = din("pos", [1, S], I32)
    wmod_d = din("w_mod", [L, D, 6 * D])
    bmod_d = din("b_mod_l", [128, L * 96])
    n1g_d = din("n1g_l", [128, L * KC])
    n2g_d = din("n2g_l", [128, L * KC])
    fing_d = din("fing_l", [128, KC])
    ret_win_d = din("ret_w_in", [D, 12288])
    ret_wout_d = din("ret_w_out", [4096, D])
    conv_win_d = din("conv_w_in", [D, 3 * D])
    convw_d = din("conv_w_l", [128, 3 * KC])
    conv_wout_d = din("conv_w_out", [D, D])
    gla_win_d = din("gla_w_in", [D, 6160])
    gla_wup_d = din("gla_w_up", [16, 1024])
    gla_negb_d = din("gla_negb_l", [128, 8])
    gla_wout_d = din("gla_w_out", [D, D])
    pool_w_d = din("pool_w", [D, 512])
    pool_sc_d = din("pool_sc_l", [128, KC])
    ffn_win_d = din("ffn_w_in", [L, D, 2 * DFF])
    ffn_wout_d = din("ffn_w_out", [L, DFF, D])
    invf_d = din("inv_freq", [128, 1])
    ident_d = din("ident", [128, 128])
    m01_d = din("m01T", [128, 128])
    cpos_d = din("cpos", [1, 128])
    invc_d = din("invcnt", [4, S])
    outT_d = nc.dram_tensor("outT", [D, S], F32, kind="ExternalOutput").ap()
    xs_d = nc.dram_tensor("xs_scr", [D, S], F32).ap()
    cs_d = nc.dram_tensor("cs_scr", [2, 128, S], F32).ap()
    st_d = nc.dram_tensor("st_scr", [8, 128, 1024], F32).ap()

    xin_v = xT_d.rearrange("(kc p) t -> p kc t", p=128)
    xs_v = xs_d.rearrange("(kc p) t -> p kc t", p=128)
    out_v = outT_d.rearrange("(kc p) t -> p kc t", p=128)

    RET_LOGG = [math.log1p(-2.0 ** (-5.0 - h)) for h in range(8)]

    with ExitStack() as st:
        P = Prog(nc, st)
        A = P.sbuf("A", [128, KC, 1024], BF16)
        FLEX = P.sbuf("FLEX", [128, JC * 1024], BF16)
        wslot = [P.sbuf(f"wslot{i}", [128, SLOT], BF16) for i in range(NSLOT)]
        wsem = [P.sem(f"wsem{i}") for i in range(NSLOT)]
        xst = RR([(P.sbuf(f"xst{i}", [128, 512], F32), ("xst", i), P.sem(f"xst{i}")) for i in range(3)])
        xe = RR([(P.sbuf(f"xe{i}", [128, 512], F32), ("xe", i), P.sem(f"xel{i}"), P.sem(f"xes{i}")) for i in range(3)])
        sqb = RR([(P.sbuf(f"sqb{i}", [128, 512], BF16), ("sqb", i)) for i in range(4)])
        stf = RR([(P.sbuf(f"stf{i}", [128, 512], F32), ("stf", i)) for i in range(4)])
        rvec = P.sbuf("rvec", [128, S], F32)
        elast = P.sbuf("elast", [128, 2, 4], F32)
        ones_bf = P.sbuf("ones_bf", [128, 128], BF16)
        ident = P.sbuf("ident_sb", [128, 128], F32)
        m01 = P.sbuf("m01_sb", [128, 128], F32)
        cpos = P.sbuf("cpos_sb", [128, 128], F32)
        invf = P.sbuf("invf_sb", [128, 1], F32)
        c32 = P.sbuf("c32", [128, KC], F32)
        cact = P.sbuf("cact", [128, KC], BF16)
        bmod = P.sbuf("bmod", [128, L * 96], F32)
        modsb = P.sbuf("modsb", [128, L * 96], F32)
        n1g = P.sbuf("n1g", [128, L * KC], F32)
        n2g = P.sbuf("n2g", [128, L * KC], F32)
        fing = P.sbuf("fing", [128, KC], F32)
        avec = P.sbuf("avec", [128, 2 * L * KC], F32)
        poolsc = P.sbuf("poolsc", [128, KC], F32)
        g1p = P.sbuf("g1p", [128, KC], F32)
        convw = P.sbuf("convw", [128, 3 * KC], F32)
        negb = P.sbuf("negb", [128, 8], F32)
        uc_carry = P.sbuf("uc_carry", [128, KC, 2], F32)
        pool_carry = P.sbuf("pool_carry", [128, KC, 16], F32)
        ps = [P.psum(f"ps{i}", [128, 512]) for i in range(8)]
        psr = RR([(ps[i], ("ps", i)) for i in range(8)])
        psr_hi = RR([(ps[i], ("ps", i)) for i in range(4, 8)])
        psr_lo = RR([(ps[i], ("ps", i)) for i in range(0, 4)])
        psr6 = RR([(ps[i], ("ps", i)) for i in range(0, 6)])
        nbanks = [(ps[6], ("ps", 6)), (ps[7], ("ps", 7))]
        msem = P.sem("misc")
        msem_tok = [None]

        def mdma(eng, out, in_, reads=(), writes=()):
            msem_tok[0] = P.dma(eng, out, in_, msem, reads=reads, writes=writes, extra=[msem_tok[0]] if msem_tok[0] else [])
            return msem_tok[0]

        def fx(off, n, dt=BF16):
            v = FLEX[:, off:off + n]
            return v.bitcast(F32) if dt == F32 else v

        G_ffn = FLEX[:, :].rearrange("p (j t) -> p j t", j=JC)
        OG = fx(0, 32 * 512).rearrange("p (j t) -> p j t", j=32)
        off = 32 * 512
        off0 = off
        qdec = fx(off, 1024).rearrange("p (c t) -> p c t", c=2); off += 1024
        kdec = fx(off, 1024).rearrange("p (c t) -> p c t", c=2); off += 1024
        kctok = fx(off, 1024).rearrange("p (c d) -> p c d", c=4); off += 1024
        vtok = fx(off, 2048).rearrange("p (c v) -> p c v", c=4); off += 2048
        gsil = fx(off, 2048).rearrange("p (c t) -> p c t", c=4); off += 2048
        Smask = fx(off, 512).rearrange("p (c t) -> p c t", c=4); off += 512
        sqo = fx(off, 2048).rearrange("p (c t) -> p c t", c=4); off += 2048
        Sbf2 = [fx(off + i * 1024, 1024).rearrange("p (c t) -> p c t", c=2) for i in range(2)]; off += 2048
        bA = fx(off, 2048, F32).rearrange("p (c t) -> p c t", c=2); off += 2048
        bB = fx(off, 2048, F32).rearrange("p (c t) -> p c t", c=2); off += 2048
        Ebuf = fx(off, 2048, F32).rearrange("p (c t) -> p c t", c=2); off += 2048
        qrot = fx(off, 2048, F32).rearrange("p (c t) -> p c t", c=2); off += 2048
        krot = fx(off, 2048, F32).rearrange("p (c t) -> p c t", c=2); off += 2048
        S32 = fx(off, 2048, F32).rearrange("p (c t) -> p c t", c=2); off += 2048
        cs_sb = fx(off, 2048, F32).rearrange("p (c t) -> p c t", c=2)
        zT = fx(off, 1024, F32)[0:16, :]; off += 1024
        wup = fx(off, 2048, F32)[0:16, :]; off += 2048
        rhead = fx(off, 1024, F32); off += 1024
        assert off <= JC * 1024, off
        off = off0
        ucx = fx(off, 1056, F32)[:, 0:514]; off += 1056
        hx = RR([(fx(off + i * 1056, 1056, F32), ("hx", i)) for i in range(2)]); off += 2 * 1056
        Wp = [fx(off + i * 1056, 1056, F32) for i in range(4)]; off += 4 * 1056
        invc = fx(off, 4096, F32).rearrange("p (c t) -> p c t", c=4); off += 4096
        assert off <= JC * 1024

        plan = []
        wstate = {"issued": 0, "next": 0}

        def w_issue():
            i = wstate["issued"]
            if i >= len(plan):
                return
            s = i % NSLOT
            for src, dstf in plan[i]:
                P.dma("gpsimd", dstf(wslot[s]), src, wsem[s], writes=[("w", s)])
            wstate["issued"] += 1

        def w_get():
            i = wstate["next"]
            wstate["next"] += 1
            while wstate["issued"] < min(i + NSLOT, len(plan)) and wstate["issued"] <= i:
                w_issue()
            return wslot[i % NSLOT], ("w", i % NSLOT)

        def w_done():
            while wstate["issued"] < min(wstate["next"] - 1 + NSLOT + 1, len(plan)):
                w_issue()

        def view(slot, kcn, fb):
            return slot[:, 0:kcn * fb].rearrange("p (k f) -> p k f", f=fb)

        def blk_cols(w2d, kcn, colranges):
            wv = w2d.rearrange("(kc p) f -> p kc f", p=128)
            fb = sum(b - a for a, b in colranges)
            descs = []
            o = 0
            for a, b in colranges:
                descs.append((wv[:, :, a:b], (lambda slot, o=o, n=b - a: view(slot, kcn, fb)[:, :, o:o + n])))
                o += b - a
            return descs

        def plan_all():
            for b in range(24):
                plan.append(blk_cols(wmod_d[0], KC, [(b * 512, (b + 1) * 512)]))
            pm = {"cnt": 0, "b": 0}

            def plan_maybe_mod(l):
                if l + 1 >= L or pm["b"] >= 24:
                    return
                pm["cnt"] += 1
                if pm["cnt"] % 2 == 0:
                    plan.append(blk_cols(wmod_d[l + 1], KC, [(pm["b"] * 512, (pm["b"] + 1) * 512)]))
                    pm["b"] += 1
            sub = 0
            for l in range(L):
                if sub >= nsub:
                    break
                m = l % 4
                for q in range(4):
                    if m == 0:
                        for hd in range(8):
                            plan.append(blk_cols(ret_win_d, KC, [(hd * 256, hd * 256 + 256), (2048 + hd * 256, 2048 + hd * 256 + 256)]))
                            plan.append(blk_cols(ret_win_d, KC, [(4096 + hd * 512, 4096 + hd * 512 + 512)]))
                            plan.append(blk_cols(ret_win_d, KC, [(8192 + hd * 512, 8192 + hd * 512 + 512)]))
                        for b in range(8):
                            plan.append(blk_cols(ret_wout_d, 32, [(b * 256, b * 256 + 256)]))
                    elif m == 1:
                        for fc in range(16):
                            plan.append(blk_cols(conv_win_d, KC, [(fc * 384, fc * 384 + 384)]))
                        for b in range(4):
                            plan.append(blk_cols(conv_wout_d, KC, [(b * 512, b * 512 + 512)]))
                    elif m == 2:
                        plan.append(blk_cols(gla_win_d, KC, [(6144, 6160)]))
                        for hd in range(4):
                            plan.append(blk_cols(gla_win_d, KC, [(hd * 256, hd * 256 + 256), (1024 + hd * 256, 1024 + hd * 256 + 256)]))
                            plan.append(blk_cols(gla_win_d, KC, [(2048 + hd * 512, 2048 + hd * 512 + 512)]))
                            plan.append(blk_cols(gla_win_d, KC, [(4096 + hd * 512, 4096 + hd * 512 + 512)]))
                        for b in range(4):
                            plan.append(blk_cols(gla_wout_d, KC, [(b * 512, b * 512 + 512)]))
                    else:
                        for g in range(4):
                            plan.append(blk_cols(pool_w_d[g * 512:(g + 1) * 512, :], 4, [(0, 512)]))
                sub += 1
                if sub >= nsub:
                    break
                pm["cnt"], pm["b"] = 0, 0
                for hf in range(2):
                    for b in range(22):
                        plan.append(blk_cols(ffn_win_d[l], KC, [(b * 256, b * 256 + 256), (DFF + b * 256, DFF + b * 256 + 256)]))
                        plan_maybe_mod(l)
                    for i in range(16):
                        plan.append(blk_cols(ffn_wout_d[l], JC, [(i * 128, i * 128 + 128)]))
                        plan_maybe_mod(l)
                sub += 1

        plan_all()

        def act(fn, reads=(), writes=()):
            return P.op("scalar", fn, reads=reads, writes=writes)

        def dve(fn, reads=(), writes=()):
            return P.op("vector", fn, reads=reads, writes=writes)

        def pe(fn, reads=(), writes=(), mark=False):
            return P.op("tensor", fn, reads=reads, writes=writes, mark=mark)

        def rsqrt_from_psum(pb, pk, dst, dkey, scale):
            dve(lambda e: e.tensor_scalar(out=dst, in0=pb[:], scalar1=scale, scalar2=EPS, op0=ALU.mult, op1=ALU.add), reads=[pk], writes=[dkey])
            act(lambda e: e.activation(out=dst, in_=dst, func=AF.Sqrt), reads=[dkey], writes=[dkey])
            dve(lambda e: e.reciprocal(out=dst, in_=dst), reads=[dkey], writes=[dkey])

        def rv(tbg):
            return rvec[:, tbg * 512:(tbg + 1) * 512]

        def norm_sums(t0, tt, src):
            for tb in range(tt // 512):
                pb, pk = psr.next()
                ta = t0 + tb * 512
                for kc in range(KC):
                    xb, xk, xsem = xst.next()
                    P.dma("sync", xb[:], src[:, kc, ta:ta + 512], xsem, reads=[("x", kc, ta // 512)], writes=[xk])
                    sb, sk = sqb.next()
                    act(lambda e, sb=sb, xb=xb: e.activation(out=sb[:], in_=xb[:], func=AF.Square), reads=[xk], writes=[sk])
                    pe(lambda e, pb=pb, sb=sb, kc=kc: e.matmul(pb[:], ones_bf[:], sb[:], start=(kc == 0), stop=(kc == KC - 1)),
                       reads=[sk, "ones"], writes=[pk], mark=True)
                rsqrt_from_psum(pb, pk, rv(ta // 512), ("rvec", ta // 512), 1.0 / D)

        def prologue(t0, tt, a_ap, s_ap, src):
            for tb in range(tt // 512):
                ta = t0 + tb * 512
                tbg = ta // 512
                sl = slice(tb * 512, (tb + 1) * 512)
                for kc in range(KC):
                    xb, xk, xsem = xst.next()
                    P.dma("sync", xb[:], src[:, kc, ta:ta + 512], xsem, reads=[("x", kc, tbg)], writes=[xk])
                    dve(lambda e, xb=xb, kc=kc, tbg=tbg: e.scalar_tensor_tensor(out=xb[:], in0=xb[:], scalar=a_ap[:, kc:kc + 1], in1=rv(tbg),
                                                                                op0=ALU.mult, op1=ALU.mult), reads=[xk, "vecs", ("rvec", tbg)], writes=[xk])
                    act(lambda e, xb=xb, kc=kc, sl=sl: e.activation(out=A[:, kc, sl], in_=xb[:], func=AF.Identity, bias=s_ap[:, kc:kc + 1], scale=1.0),
                        reads=[xk, "vecs"], writes=[("h", kc, tb)])
                    yield

        class Stepper:
            def __init__(self, gen, total):
                self.gen, self.total, self.started = gen, total, False

            def start(self):
                self.started = True

            def step(self, ngroups):
                if not self.started or self.gen is None:
                    return
                for _ in range(-(-self.total // ngroups)):
                    if next(self.gen, "end") == "end":
                        self.gen = None
                        return

            def finish(self):
                if self.gen is not None:
                    for _ in self.gen:
                        pass
                    self.gen = None

        def gemm_fm(pb, pk, wv, wk, fsl, actv, akeys, kcn, tsl, mark=True):
            for kc in range(kcn):
                pe(lambda e, kc=kc: e.matmul(pb, wv[:, kc, fsl], actv[:, kc, tsl], start=(kc == 0), stop=(kc == kcn - 1)),
                   reads=[wk, akeys[kc]], writes=[pk], mark=(mark and kc == kcn - 1))

        def x_epilogue(pb, pk, gate_col, fc, tbg, src, nb):
            eb, ek, lsem, ssem = xe.next()
            P.dma("sync", eb[:], src[:, fc, tbg * 512:(tbg + 1) * 512], lsem, reads=[("x", fc, tbg)], writes=[ek])
            dve(lambda e: e.scalar_tensor_tensor(out=eb[:], in0=pb[:], scalar=gate_col, in1=eb[:], op0=ALU.mult, op1=ALU.add),
                reads=[pk, ek, "vecs"], writes=[ek])
            P.dma("sync", xs_v[:, fc, tbg * 512:(tbg + 1) * 512], eb[:], ssem, reads=[ek], writes=[("x", fc, tbg)])
            sb, sk = sqb.next()
            act(lambda e: e.activation(out=sb[:], in_=eb[:], func=AF.Square), reads=[ek], writes=[sk])
            nbp, nbk = nb

            def deferred():
                pe(lambda e: e.matmul(nbp[:], ones_bf[:], sb[:], start=(fc == 0), stop=(fc == KC - 1)), reads=[sk, "ones"], writes=[nbk], mark=True)
                if fc == KC - 1:
                    rsqrt_from_psum(nbp, nbk, rv(tbg), ("rvec", tbg), 1.0 / D)
            return deferred

        def outproj(nblk, kcn, fb, actv, akeys, gate_ap, t0, tt, src, fc_of=None, after_block=None, stepper=None):
            ntb = tt // 512
            nfc = fb // 128
            ngroups = nblk * nfc * ntb
            defq = []
            for b in range(nblk):
                slot, wk = w_get()
                wv = view(slot, kcn, fb)
                for f in range(nfc):
                    fc = b * nfc + f if fc_of is None else fc_of(b, f)
                    for tb in range(ntb):
                        pb, pk = psr6.next()
                        a_v, a_k = (actv, akeys) if fc_of is None else actv(b)
                        gemm_fm(pb[:], pk, wv, wk, slice(f * 128, (f + 1) * 128), a_v, a_k, kcn, slice(tb * 512, (tb + 1) * 512))
                        defq.append(x_epilogue(pb, pk, gate_ap[:, fc:fc + 1], fc, t0 // 512 + tb, src, nbanks[tb]))
                        if len(defq) > 2:
                            defq.pop(0)()
                        if stepper is not None:
                            stepper.step(ngroups)
                w_done()
                if after_block is not None:
                    after_block()
            for d_ in defq:
                d_()

        for dst, src, key in [(ident, ident_d, "ident"), (m01, m01_d, "m01"), (invf, invf_d, "invf"), (c32, c_d, "c32"), (bmod, bmod_d, "bmod"),
                              (n1g, n1g_d, "n1g"), (n2g, n2g_d, "n2g"), (fing, fing_d, "fing"), (poolsc, pool_sc_d, "poolsc"),
                              (convw, convw_d, "vecs0"), (negb, gla_negb_d, "negb")]:
            mdma("sync", dst[:], src, writes=[key])
        mdma("sync", cpos[:], cpos_d.partition_broadcast(128), writes=["cpos"])
        dve(lambda e: e.memset(ones_bf[:], 1.0), writes=["ones"])
        dve(lambda e: e.memset(uc_carry[:], 0.0), writes=["uc_carry"])
        dve(lambda e: e.memset(pool_carry[:], 0.0), writes=["pool_carry"])
        act(lambda e: e.activation(out=cact[:], in_=c32[:], func=AF.Silu), reads=["c32"], writes=["cact"])
        HI = 6.28125
        LO = 2 * PI - HI
        posi = xst.items[0][0].bitcast(I32)
        angb = xst.items[1][0]
        an = xst.items[2][0]
        nI = stf.items[0][0].bitcast(I32)
        nF = stf.items[1][0]
        rr_ = stf.items[2][0]
        kposi, kang, kan, knI, knF, krr = ("xst", 0), ("xst", 1), ("xst", 2), ("stf", 0), ("stf", 1), ("stf", 2)
        for blk in range(S // 512):
            tsl = slice(blk * 512, (blk + 1) * 512)
            mdma("sync", posi[:], pos_d[:, tsl].partition_broadcast(128), writes=[kposi])
            dve(lambda e: e.tensor_copy(out=angb[:], in_=posi[:]), reads=[kposi], writes=[kang])
            dve(lambda e: e.tensor_scalar(out=angb[:], in0=angb[:], scalar1=invf[:, 0:1], scalar2=None, op0=ALU.mult), reads=[kang, "invf"], writes=[kang])
            for which, shift in ((0, 0.5 * PI), (1, 0.0)):
                dve(lambda e, shift=shift: e.tensor_scalar(out=an[:], in0=angb[:], scalar1=shift, scalar2=None, op0=ALU.add), reads=[kang], writes=[kan])
                dve(lambda e: e.tensor_scalar(out=nF[:], in0=an[:], scalar1=1.0 / (2 * PI), scalar2=None, op0=ALU.mult), reads=[kan], writes=[knF])
                dve(lambda e: e.tensor_copy(out=nI[:], in_=nF[:]), reads=[knF], writes=[knI])
                dve(lambda e: e.tensor_copy(out=nF[:], in_=nI[:]), reads=[knI], writes=[knF])
                dve(lambda e: e.scalar_tensor_tensor(out=rr_[:], in0=nF[:], scalar=-HI, in1=an[:], op0=ALU.mult, op1=ALU.add), reads=[knF, kan], writes=[krr])
                dve(lambda e: e.scalar_tensor_tensor(out=rr_[:], in0=nF[:], scalar=-LO, in1=rr_[:], op0=ALU.mult, op1=ALU.add), reads=[knF, krr], writes=[krr])
                dve(lambda e: e.tensor_scalar(out=nF[:], in0=rr_[:], scalar1=PI, scalar2=2 * PI, op0=ALU.is_gt, op1=ALU.mult), reads=[krr], writes=[knF])
                dve(lambda e: e.tensor_tensor(out=rr_[:], in0=rr_[:], in1=nF[:], op=ALU.subtract), reads=[knF, krr], writes=[krr])
                dve(lambda e: e.tensor_scalar(out=nF[:], in0=rr_[:], scalar1=-PI, scalar2=2 * PI, op0=ALU.is_lt, op1=ALU.mult), reads=[krr], writes=[knF])
                dve(lambda e: e.tensor_tensor(out=rr_[:], in0=rr_[:], in1=nF[:], op=ALU.add), reads=[knF, krr], writes=[krr])
                act(lambda e: e.activation(out=rr_[:], in_=rr_[:], func=AF.Sin), reads=[krr], writes=[krr])
                mdma("sync", cs_d[which, :, tsl], rr_[:], reads=[krr], writes=[("cs", which)])
        def mod(l, j):
            return modsb[:, l * 96 + j * 16: l * 96 + (j + 1) * 16]

        def mod_block(l, b):
            slot, wk = w_get()
            wv = view(slot, KC, 512)
            pb, pk = psr6.next()
            for f in range(4):
                for kc in range(KC):
                    pe(lambda e, f=f, kc=kc: e.matmul(pb[:, f:f + 1], wv[:, kc, f * 128:(f + 1) * 128], cact[:, kc:kc + 1], start=(kc == 0), stop=(kc == KC - 1)),
                       reads=[wk, "cact"], writes=[pk], mark=(f == 3 and kc == KC - 1))
            w_done()
            c0 = l * 96 + b * 4
            dve(lambda e: e.tensor_tensor(out=modsb[:, c0:c0 + 4], in0=pb[:, 0:4], in1=bmod[:, c0:c0 + 4], op=ALU.add), reads=[pk, "bmod"], writes=["modsb"])

        def mod_finish(l):
            for which, (gsrc, jsc) in enumerate(((n1g, 1), (n2g, 4))):
                dst = avec[:, (l * 2 + which) * KC:(l * 2 + which + 1) * KC]
                dve(lambda e, dst=dst, jsc=jsc: e.tensor_scalar(out=dst, in0=mod(l, jsc), scalar1=1.0, scalar2=None, op0=ALU.add), reads=["modsb"], writes=["vecs"])
                dve(lambda e, dst=dst, gsrc=gsrc: e.tensor_tensor(out=dst, in0=dst, in1=gsrc[:, l * KC:(l + 1) * KC], op=ALU.mult), reads=["vecs", "n1g", "n2g"], writes=["vecs"])
            if l == 3:
                dve(lambda e: e.tensor_tensor(out=g1p[:], in0=mod(3, 2), in1=poolsc[:], op=ALU.mult), reads=["modsb", "poolsc"], writes=["vecs"])

        norm_sums(0, S, xin_v)
        for b in range(24):
            mod_block(0, b)
        mod_finish(0)
        dve(lambda e: e.tensor_copy(out=convw[:], in_=convw[:]), reads=["vecs0"], writes=["vecs"])
        modst = {"cnt": 0, "b": 0}

        def maybe_mod(l):
            if l + 1 >= L or modst["b"] >= 24:
                return
            modst["cnt"] += 1
            if modst["cnt"] % 2 == 0:
                mod_block(l + 1, modst["b"])
                modst["b"] += 1
                if modst["b"] == 24:
                    mod_finish(l + 1)

        def mixer_pro(l, t0, src):
            return Stepper(prologue(t0, 512, avec[:, (l * 2) * KC:(l * 2 + 1) * KC], mod(l, 0), src), 16)

        def ffn_pro(l, hf):
            return Stepper(prologue(hf * 1024, 1024, avec[:, (l * 2 + 1) * KC:(l * 2 + 2) * KC], mod(l, 3), xs_v), 32)

        def lin_attn(l, kind, q_, t0, src, hook):
            TT = 512
            nh = 8 if kind == "ret" else 4
            hk = [("h", kc, 0) for kc in range(KC)]
            bA4 = bA.rearrange("p c (n t) -> p c n t", n=4)
            bB4 = bB.rearrange("p c (n t) -> p c n t", n=4)
            E4 = Ebuf.rearrange("p c (n t) -> p c n t", n=4)
            krot4 = krot.rearrange("p c (n t) -> p c n t", n=4)
            if kind == "ret":
                mdma("sync", cs_sb[:, 0, :], cs_d[0, :, t0:t0 + TT], reads=[("cs", 0)], writes=["cs_sb"])
                mdma("sync", cs_sb[:, 1, :], cs_d[1, :, t0:t0 + TT], reads=[("cs", 1)], writes=["cs_sb"])
            else:
                mdma("sync", wup, gla_wup_d, writes=["wup"])
                slot, wk = w_get()
                wv = view(slot, KC, 16)
                pb, pk = psr.next()
                gemm_fm(pb[0:16, :], pk, wv, wk, slice(0, 16), A, hk, KC, slice(0, TT))
                w_done()
                act(lambda e, pb=pb: e.activation(out=zT, in_=pb[0:16, :], func=AF.Copy), reads=[pk], writes=["zT"])
            for hd in range(nh):
                if kind == "ret":
                    dve(lambda e, hd=hd: e.tensor_scalar(out=bA4, in0=cpos[:, :].unsqueeze(1).unsqueeze(1).to_broadcast([128, 2, 4, 128]), scalar1=RET_LOGG[hd],
                                                         scalar2=None, op0=ALU.mult), reads=["cpos"], writes=["bA"])
                    bfin, bkey = bA, "bA"
                else:
                    for dc in range(2):
                        pb, pk = psr.next()
                        pe(lambda e, pb=pb, hd=hd, dc=dc: e.matmul(pb[:], wup[:, (hd * 2 + dc) * 128:(hd * 2 + dc + 1) * 128], zT, start=True, stop=True),
                           reads=["wup", "zT"], writes=[pk], mark=True)
                        act(lambda e, pb=pb, hd=hd, dc=dc: e.activation(out=bA[:, dc, :], in_=pb[:], func=AF.Exp, bias=negb[:, hd * 2 + dc:hd * 2 + dc + 1], scale=-1.0),
                            reads=[pk, "negb"], writes=["bA"])
                    act(lambda e: e.activation(out=bA, in_=bA, func=AF.Ln, bias=1.0, scale=1.0), reads=["bA"], writes=["bA"])
                    dve(lambda e: e.tensor_scalar(out=bA, in0=bA, scalar1=-1.0 / 16.0, scalar2=None, op0=ALU.mult), reads=["bA"], writes=["bA"])
                    src4, dst4, sk_, dk_ = bA4, bB4, "bA", "bB"
                    for sh in (1, 2, 4, 8, 16, 32, 64):
                        dve(lambda e, src4=src4, dst4=dst4, sh=sh: e.tensor_tensor(out=dst4[:, :, :, sh:], in0=src4[:, :, :, sh:], in1=src4[:, :, :, :128 - sh], op=ALU.add),
                            reads=[sk_], writes=[dk_])
                        dve(lambda e, src4=src4, dst4=dst4, sh=sh: e.tensor_copy(out=dst4[:, :, :, :sh], in_=src4[:, :, :, :sh]), reads=[sk_], writes=[dk_])
                        src4, dst4, sk_, dk_ = dst4, src4, dk_, sk_
                    bfin, bkey = bB, "bB"
                slot, wk = w_get()
                wv = view(slot, KC, 512)
                banks = [psr_hi.next() for _ in range(4)]
                for f, (pb, pk) in enumerate(banks):
                    gemm_fm(pb[:], pk, wv, wk, slice(f * 128, (f + 1) * 128), A, hk, KC, slice(0, TT))
                w_done()
                if kind == "ret":
                    for (b0, b1), dst, dkey, sc in (((banks[0], banks[1]), qrot, "qrot", 1.0), ((banks[2], banks[3]), krot, "krot", 1.0 / 16.0)):
                        t1, k1 = stf.next()
                        t2, k2 = stf.next()
                        m1, km1 = stf.next()
                        m2, km2 = stf.next()
                        act(lambda e, t1=t1, b0=b0, sc=sc: e.activation(out=t1[:], in_=b0[0][:], func=AF.Copy, scale=sc), reads=[b0[1]], writes=[k1])
                        act(lambda e, t2=t2, b1=b1, sc=sc: e.activation(out=t2[:], in_=b1[0][:], func=AF.Copy, scale=sc), reads=[b1[1]], writes=[k2])
                        dve(lambda e, t1=t1, m1=m1: e.tensor_tensor(out=m1[:], in0=t1[:], in1=cs_sb[:, 0, :], op=ALU.mult), reads=[k1, "cs_sb"], writes=[km1])
                        dve(lambda e, t2=t2, m2=m2: e.tensor_tensor(out=m2[:], in0=t2[:], in1=cs_sb[:, 1, :], op=ALU.mult), reads=[k2, "cs_sb"], writes=[km2])
                        dve(lambda e, m1=m1, m2=m2, dst=dst: e.tensor_tensor(out=dst[:, 0, :], in0=m1[:], in1=m2[:], op=ALU.subtract), reads=[km1, km2], writes=[dkey])
                        dve(lambda e, t2=t2, m1=m1: e.tensor_tensor(out=m1[:], in0=t2[:], in1=cs_sb[:, 0, :], op=ALU.mult), reads=[k2, "cs_sb"], writes=[km1])
                        dve(lambda e, t1=t1, m2=m2: e.tensor_tensor(out=m2[:], in0=t1[:], in1=cs_sb[:, 1, :], op=ALU.mult), reads=[k1, "cs_sb"], writes=[km2])
                        dve(lambda e, m1=m1, m2=m2, dst=dst: e.tensor_tensor(out=dst[:, 1, :], in0=m1[:], in1=m2[:], op=ALU.add), reads=[km1, km2], writes=[dkey])
                else:
                    for f, (dst, dkey, sc) in enumerate(((qrot, "qrot", 1.0 / 16.0), (qrot, "qrot", 1.0 / 16.0), (krot, "krot", 1.0), (krot, "krot", 1.0))):
                        pb, pk = banks[f]
                        act(lambda e, pb=pb, dst=dst, f=f, sc=sc: e.activation(out=dst[:, f % 2, :], in_=pb[:], func=AF.Copy, scale=sc), reads=[pk], writes=[dkey])
                act(lambda e, bfin=bfin: e.activation(out=Ebuf, in_=bfin, func=AF.Exp), reads=[bkey], writes=["E"])
                dve(lambda e: e.tensor_tensor(out=qdec, in0=qrot, in1=Ebuf, op=ALU.mult), reads=["qrot", "E"], writes=["qdec"])
                dve(lambda e: e.tensor_copy(out=elast[:, :, :].unsqueeze(3), in_=E4[:, :, :, 127:128]), reads=["E"], writes=["elast"])
                act(lambda e, bfin=bfin: e.activation(out=Ebuf, in_=bfin, func=AF.Exp, scale=-1.0), reads=[bkey], writes=["E"])
                dve(lambda e: e.tensor_tensor(out=krot, in0=krot, in1=Ebuf, op=ALU.mult), reads=["krot", "E"], writes=["krot"])
                act(lambda e: e.activation(out=kdec, in_=krot, func=AF.Copy), reads=["krot"], writes=["kdec"])
                dve(lambda e: e.tensor_tensor(out=krot4, in0=krot4, in1=elast[:, :, :].unsqueeze(3).to_broadcast([128, 2, 4, 128]), op=ALU.mult),
                    reads=["krot", "elast"], writes=["krot"])
                slot, wk = w_get()
                wv = view(slot, KC, 512)
                for tt_ in range(4):
                    pb, pk = psr_lo.next()
                    for kc in range(KC):
                        pe(lambda e, pb=pb, wv=wv, kc=kc, tt_=tt_: e.matmul(pb[:], A[:, kc, tt_ * 128:(tt_ + 1) * 128], wv[:, kc, :], start=(kc == 0), stop=(kc == KC - 1)),
                           reads=[wk, ("h", kc, 0)], writes=[pk], mark=(kc == KC - 1))
                    if tt_ % 2 == 0:
                        act(lambda e, pb=pb, tt_=tt_: e.activation(out=vtok[:, tt_, :], in_=pb[:], func=AF.Copy), reads=[pk], writes=["vtok"])
                    else:
                        dve(lambda e, pb=pb, tt_=tt_: e.tensor_copy(out=vtok[:, tt_, :], in_=pb[:]), reads=[pk], writes=["vtok"])
                w_done()
                slot, wk = w_get()
                wv = view(slot, KC, 512)
                for f in range(4):
                    pb, pk = psr_lo.next()
                    gemm_fm(pb[:], pk, wv, wk, slice(f * 128, (f + 1) * 128), A, hk, KC, slice(0, TT))
                    act(lambda e, pb=pb, f=f: e.activation(out=gsil[:, f, :], in_=pb[:], func=AF.Silu), reads=[pk], writes=["gsil"])
                w_done()
                if hd == nh - 1:
                    hook.start()
                for c2 in range(2):
                    pb, pk = psr_hi.next()
                    for cc in range(2):
                        c = c2 * 2 + cc
                        for dc in range(2):
                            pe(lambda e, pb=pb, cc=cc, dc=dc, c=c: e.transpose(pb[:, cc * 256 + dc * 128: cc * 256 + (dc + 1) * 128], krot[:, dc, c * 128:(c + 1) * 128], ident[:]),
                               reads=["krot", "ident"], writes=[pk], mark=(cc == 1 and dc == 1))
                    act(lambda e, pb=pb, c2=c2: e.activation(out=kctok[:, c2 * 2:c2 * 2 + 2, :], in_=pb[:].rearrange("p (c d) -> p c d", c=2), func=AF.Copy),
                        reads=[pk], writes=["kctok"])
                if q_ == 0:
                    dve(lambda e: e.memset(S32, 0.0), writes=["S32"])
                else:
                    mdma("sync", S32, st_d[hd].rearrange("p (c v) -> p c v", c=2), reads=[("st", hd)], writes=["S32"])
                act(lambda e: e.activation(out=Sbf2[0], in_=S32, func=AF.Copy), reads=["S32"], writes=[("Sbf", 0)])
                pb, pk = psr_hi.next()
                for c in range(4):
                    csl = slice(c * 128, (c + 1) * 128)
                    for dc in range(2):
                        pe(lambda e, pb=pb, csl=csl, dc=dc: e.matmul(pb[:, csl], kdec[:, dc, csl], qdec[:, dc, csl], start=(dc == 0), stop=(dc == 1)),
                           reads=["kdec", "qdec"], writes=[pk], mark=(c == 3 and dc == 1))
                dve(lambda e, pb=pb: e.tensor_tensor(out=Smask, in0=pb[:].rearrange("p (c t) -> p c t", c=4), in1=m01[:, :].unsqueeze(1).to_broadcast([128, 4, 128]), op=ALU.mult),
                    reads=[pk, "m01"], writes=["Smask"])
                okeys = [("ps", vc) for vc in range(4)]
                for c in range(4):
                    csl = slice(c * 128, (c + 1) * 128)
                    Scur, kcur = Sbf2[c % 2], ("Sbf", c % 2)
                    Snxt, knxt = Sbf2[(c + 1) % 2], ("Sbf", (c + 1) % 2)
                    kvb = []
                    for dc in range(2):
                        pb, pk = psr_hi.next()
                        kvb.append((pb, pk))
                        pe(lambda e, pb=pb, c=c, dc=dc: e.matmul(pb[:], kctok[:, c, dc * 128:(dc + 1) * 128], vtok[:, c, :], start=True, stop=True),
                           reads=["kctok", "vtok"], writes=[pk], mark=True)
                    for vc in range(4):
                        vsl = slice(vc * 128, (vc + 1) * 128)
                        pe(lambda e, vc=vc, c=c, csl=csl, vsl=vsl: e.matmul(ps[vc][:, csl], vtok[:, c, vsl], Smask[:, c, :], start=True, stop=False),
                           reads=["vtok", "Smask"], writes=[okeys[vc]])
                        pe(lambda e, vc=vc, csl=csl, vsl=vsl, Scur=Scur: e.matmul(ps[vc][:, csl], Scur[:, 0, vsl], qdec[:, 0, csl], start=False, stop=False),
                           reads=[kcur, "qdec"], writes=[okeys[vc]])
                        pe(lambda e, vc=vc, csl=csl, vsl=vsl, Scur=Scur: e.matmul(ps[vc][:, csl], Scur[:, 1, vsl], qdec[:, 1, csl], start=False, stop=True),
                           reads=[kcur, "qdec"], writes=[okeys[vc]], mark=(vc == 3))
                    for dc in range(2):
                        pb, pk = kvb[dc]
                        dve(lambda e, pb=pb, c=c, dc=dc: e.scalar_tensor_tensor(out=S32[:, dc, :], in0=S32[:, dc, :], scalar=elast[:, dc, c:c + 1], in1=pb[:],
                                                                                op0=ALU.mult, op1=ALU.add), reads=[pk, "S32", "elast"], writes=["S32"])
                    if c < 3:
                        act(lambda e, Snxt=Snxt: e.activation(out=Snxt, in_=S32, func=AF.Copy), reads=["S32"], writes=[knxt])
                mdma("sync", st_d[hd].rearrange("p (c v) -> p c v", c=2), S32, reads=["S32"], writes=[("st", hd)])
                for vc in range(4):
                    act(lambda e, vc=vc: e.activation(out=sqo[:, vc, :], in_=ps[vc][:], func=AF.Square), reads=[okeys[vc]], writes=["sqo"])
                pb, pk = psr_hi.next()
                for vc in range(4):
                    pe(lambda e, pb=pb, vc=vc: e.matmul(pb[:], ones_bf[:], sqo[:, vc, :], start=(vc == 0), stop=(vc == 3)), reads=["sqo", "ones"], writes=[pk], mark=(vc == 3))
                rsqrt_from_psum(pb, pk, rhead, "rhead", 1.0 / 512.0)
                for vc in range(4):
                    tb_, ktb = stf.next()
                    dve(lambda e, vc=vc, tb_=tb_: e.tensor_tensor(out=tb_[:], in0=ps[vc][:], in1=rhead, op=ALU.mult), reads=[okeys[vc], "rhead"], writes=[ktb])
                    dve(lambda e, vc=vc, hd=hd, tb_=tb_: e.tensor_tensor(out=OG[:, hd * 4 + vc, :], in0=tb_[:], in1=gsil[:, vc, :], op=ALU.mult),
                        reads=[ktb, "gsil"], writes=[("og", hd * 4 + vc)])
            kcn = nh * 4
            outproj(8 if kind == "ret" else 4, kcn, SLOT // kcn, OG, [("og", j) for j in range(kcn)], mod(l, 2), t0, TT, src, stepper=hook)

        def conv_mixer(l, q_, t0, src, hook):
            TT = 512
            hk = [("h", kc, 0) for kc in range(KC)]
            YB = OG
            for fc in range(KC):
                slot, wk = w_get()
                wv = view(slot, KC, 384)
                banks = [psr.next() for _ in range(3)]
                for g, (pb, pk) in enumerate(banks):
                    gemm_fm(pb[:], pk, wv, wk, slice(g * 128, (g + 1) * 128), A, hk, KC, slice(0, TT))
                w_done()
                (pbb, pkb), (pbc, pkc), (pbu, pku) = banks
                cs_, kcs = stf.next()
                y_, ky = stf.next()
                act(lambda e, cs_=cs_, pbc=pbc: e.activation(out=cs_[:], in_=pbc[:], func=AF.Copy), reads=[pkc], writes=[kcs])
                act(lambda e, fc=fc: e.activation(out=ucx[:, 0:2], in_=uc_carry[:, fc, :], func=AF.Copy), reads=["uc_carry"], writes=["ucx"])
                dve(lambda e, cs_=cs_, pbu=pbu: e.tensor_tensor(out=ucx[:, 2:514], in0=cs_[:], in1=pbu[:], op=ALU.mult), reads=[kcs, pku, "ucx"], writes=["ucx"])
                act(lambda e, y_=y_, fc=fc: e.activation(out=y_[:], in_=ucx[:, 2:514], func=AF.Copy, scale=convw[:, 2 * KC + fc:2 * KC + fc + 1]), reads=["ucx", "vecs"], writes=[ky])
                dve(lambda e, y_=y_, fc=fc: e.scalar_tensor_tensor(out=y_[:], in0=ucx[:, 1:513], scalar=convw[:, KC + fc:KC + fc + 1], in1=y_[:], op0=ALU.mult, op1=ALU.add),
                    reads=["ucx", ky, "vecs"], writes=[ky])
                dve(lambda e, y_=y_, fc=fc: e.scalar_tensor_tensor(out=y_[:], in0=ucx[:, 0:512], scalar=convw[:, fc:fc + 1], in1=y_[:], op0=ALU.mult, op1=ALU.add),
                    reads=["ucx", ky, "vecs"], writes=[ky])
                dve(lambda e, y_=y_, fc=fc, pbb=pbb: e.tensor_tensor(out=YB[:, fc, :], in0=y_[:], in1=pbb[:], op=ALU.mult), reads=[ky, pkb], writes=[("og", fc)])
                act(lambda e, fc=fc: e.activation(out=uc_carry[:, fc, :], in_=ucx[:, 512:514], func=AF.Copy), reads=["ucx"], writes=["uc_carry"])
            hook.start()
            outproj(4, KC, 512, YB, [("og", j) for j in range(KC)], mod(l, 2), t0, TT, src, stepper=hook)

        def pool_mixer(l, q_, t0, xsrc_ap, hook):
            TT = 512
            a_ap, s_ap = avec[:, (l * 2) * KC:(l * 2 + 1) * KC], mod(l, 0)
            MX = OG
            tbg = t0 // 512
            mdma("sync", invc[:], invc_d[:, t0:t0 + TT].partition_broadcast(128), writes=["invc"])
            n = 16 + TT
            for kp in range(0, KC, 2):
                gi = kp // 4
                ch = []
                for ci, kc in enumerate((kp, kp + 1)):
                    xb, xk, xsem = xst.next()
                    P.dma("sync", xb[:], xsrc_ap[:, kc, t0:t0 + TT], xsem, reads=[("x", kc, tbg)], writes=[xk])
                    hb, hkey = hx.next()
                    ch.append(dict(kc=kc, xb=xb, xk=xk, hb=hb, hkey=hkey, src=hb, skey=hkey, ci=ci))
                for c_ in ch:
                    dve(lambda e, xb=c_["xb"], kc=c_["kc"]: e.scalar_tensor_tensor(out=xb[:], in0=xb[:], scalar=a_ap[:, kc:kc + 1], in1=rv(tbg), op0=ALU.mult, op1=ALU.mult),
                        reads=[c_["xk"], "vecs", ("rvec", tbg)], writes=[c_["xk"]])
                for c_ in ch:
                    act(lambda e, hb=c_["hb"], kc=c_["kc"]: e.activation(out=hb[:, 0:16], in_=pool_carry[:, kc, :], func=AF.Copy), reads=["pool_carry"], writes=[c_["hkey"]])
                    act(lambda e, hb=c_["hb"], xb=c_["xb"], kc=c_["kc"]: e.activation(out=hb[:, 16:16 + TT], in_=xb[:], func=AF.Identity, bias=s_ap[:, kc:kc + 1], scale=1.0),
                        reads=[c_["xk"], "vecs"], writes=[c_["hkey"]])
                    act(lambda e, hb=c_["hb"], kc=c_["kc"]: e.activation(out=pool_carry[:, kc, :], in_=hb[:, TT:TT + 16], func=AF.Copy), reads=[c_["hkey"]], writes=["pool_carry"])
                for si in range(gi + 1):
                    m = 1 << si
                    for c_ in ch:
                        dst, dkey = Wp[c_["ci"] * 2 + si % 2], ("Wp", c_["ci"] * 2 + si % 2)
                        dve(lambda e, src=c_["src"], dst=dst, m=m: e.tensor_tensor(out=dst[:, m:n], in0=src[:, m:n], in1=src[:, 0:n - m], op=ALU.add), reads=[c_["skey"]], writes=[dkey])
                        c_["src"], c_["skey"] = dst, dkey
                for c_ in ch:
                    dve(lambda e, src=c_["src"], gi=gi: e.tensor_tensor(out=src[:, 16:n], in0=src[:, 16:n], in1=invc[:, gi, :], op=ALU.mult), reads=[c_["skey"], "invc"], writes=[c_["skey"]])
                for c_ in ch:
                    dve(lambda e, src=c_["src"], hb=c_["hb"], kc=c_["kc"]: e.tensor_tensor(out=MX[:, kc, :], in0=src[:, 16:n], in1=hb[:, 16:n], op=ALU.subtract),
                        reads=[c_["skey"], c_["hkey"]], writes=[("og", c_["kc"])])
            outproj(4, 4, 512, lambda b: (MX[:, b * 4:(b + 1) * 4, :], [("og", b * 4 + j) for j in range(4)]), None, g1p, t0, TT, xsrc_ap, fc_of=lambda b, f: b * 4 + f)

        def ffn(l, hf, hook):
            TT = 1024
            t0 = hf * TT
            if hf == 0:
                modst["cnt"], modst["b"] = 0, 0
            for b in range(22):
                slot, wk = w_get()
                wv = view(slot, KC, 512)
                for jj in range(2):
                    j = b * 2 + jj
                    for tb in range(2):
                        tsl = slice(tb * 512, (tb + 1) * 512)
                        hk = [("h", kc, tb) for kc in range(KC)]
                        pa, ka = psr.next()
                        pbb, kb = psr.next()
                        gemm_fm(pa[:], ka, wv, wk, slice(jj * 128, (jj + 1) * 128), A, hk, KC, tsl)
                        gemm_fm(pbb[:], kb, wv, wk, slice(256 + jj * 128, 256 + (jj + 1) * 128), A, hk, KC, tsl)
                        sb, sk = stf.next()
                        act(lambda e, sb=sb, pa=pa: e.activation(out=sb[:], in_=pa[:], func=AF.Silu), reads=[ka], writes=[sk])
                        dve(lambda e, sb=sb, pbb=pbb, j=j, tsl=tsl: e.tensor_tensor(out=G_ffn[:, j, tsl], in0=sb[:], in1=pbb[:], op=ALU.mult), reads=[sk, kb], writes=[("g", j)])
                w_done()
                maybe_mod(l)
            hook.start()
            outproj(16, JC, 128, G_ffn, [("g", j) for j in range(JC)], mod(l, 5), t0, TT, xs_v, after_block=lambda: maybe_mod(l), stepper=hook)

        slabs = []
        sub = 0
        for l in range(L):
            if sub >= nsub:
                break
            m = l % 4
            src = xin_v if l == 0 else xs_v
            for q_ in range(4):
                t0 = q_ * 512
                if m == 0:
                    slabs.append((lambda l=l, t0=t0, src=src: mixer_pro(l, t0, src), lambda hook, l=l, q_=q_, t0=t0, src=src: lin_attn(l, "ret", q_, t0, src, hook)))
                elif m == 1:
                    slabs.append((lambda l=l, t0=t0, src=src: mixer_pro(l, t0, src), lambda hook, l=l, q_=q_, t0=t0, src=src: conv_mixer(l, q_, t0, src, hook)))
                elif m == 2:
                    slabs.append((lambda l=l, t0=t0, src=src: mixer_pro(l, t0, src), lambda hook, l=l, q_=q_, t0=t0, src=src: lin_attn(l, "gla", q_, t0, src, hook)))
                else:
                    slabs.append((None, lambda hook, l=l, q_=q_, t0=t0, src=src: pool_mixer(l, q_, t0, src, hook)))
            sub += 1
            if sub >= nsub:
                break
            for hf in range(2):
                slabs.append((lambda l=l, hf=hf: ffn_pro(l, hf), lambda hook, l=l, hf=hf: ffn(l, hf, hook)))
            sub += 1
        class NoStep:
            def start(self):
                pass

            def step(self, n):
                pass

            def finish(self):
                pass

        cur = slabs[0][0]() if (slabs and slabs[0][0] is not None) else None
        if cur is not None:
            cur.finish()
        for i, (pro, body) in enumerate(slabs):
            nxt_f = slabs[i + 1][0] if i + 1 < len(slabs) else None
            st_ = nxt_f() if nxt_f is not None else NoStep()
            body(st_)
            st_.finish()
        assert wstate["next"] == len(plan), (wstate, len(plan))
        xfin = xs_v if slabs else xin_v

        out_toks = []
        osem = [P.sem("o0"), P.sem("o1"), P.sem("o2")]
        for tbg in range(4):
            ta = tbg * 512
            for kc in range(KC):
                xb, xk, xsem = xst.next()
                i = (xst.i - 1) % 3
                P.dma("sync", xb[:], xfin[:, kc, ta:ta + 512], xsem, reads=[("x", kc, tbg)], writes=[xk])
                if final:
                    dve(lambda e, xb=xb, kc=kc, tbg=tbg: e.scalar_tensor_tensor(out=xb[:], in0=xb[:], scalar=fing[:, kc:kc + 1], in1=rv(tbg),
                                                                              op0=ALU.mult, op1=ALU.mult), reads=[xk, "fing", ("rvec", tbg)], writes=[xk])
                out_toks.append(P.dma("sync", out_v[:, kc, ta:ta + 512], xb[:], osem[i], reads=[xk], writes=[("out", kc, ta)]))
        P.wait_all("sync", out_toks)
        P.run()
    return nc


def _lay(v, n):
    return np.ascontiguousarray(np.asarray(v, np.float32).reshape(n, 128).T)


def make_in_maps(inputs, cores):
    f = lambda k: np.asarray(inputs[k])
    shared = {
        "w_mod": np.ascontiguousarray(f("w_mod"), np.float32),
        "b_mod_l": np.ascontiguousarray(np.concatenate([_lay(f("b_mod")[l], 96) for l in range(L)], axis=1)),
        "n1g_l": np.ascontiguousarray(np.concatenate([_lay(f("norm1_g")[l], KC) for l in range(L)], axis=1)),
        "n2g_l": np.ascontiguousarray(np.concatenate([_lay(f("norm2_g")[l], KC) for l in range(L)], axis=1)),
        "fing_l": _lay(f("final_g"), KC),
        "ret_w_in": np.ascontiguousarray(f("ret_w_in")[0], np.float32),
        "ret_w_out": np.ascontiguousarray(f("ret_w_out")[0], np.float32),
        "conv_w_in": np.ascontiguousarray(f("conv_w_in")[0].reshape(D, 3, KC, 128).transpose(0, 2, 1, 3).reshape(D, 3 * D), np.float32),
        "conv_w_l": np.ascontiguousarray(np.concatenate([_lay(f("conv_w")[0][j], KC) for j in range(3)], axis=1)),
        "conv_w_out": np.ascontiguousarray(f("conv_w_out")[0], np.float32),
        "gla_w_in": np.ascontiguousarray(f("gla_w_in")[0], np.float32),
        "gla_w_up": np.ascontiguousarray(f("gla_w_gate_up")[0], np.float32),
        "gla_negb_l": np.ascontiguousarray(_lay(f("gla_b_gate")[0], 8) * np.float32(-1.0)),
        "gla_w_out": np.ascontiguousarray(f("gla_w_out")[0], np.float32),
        "pool_w": np.ascontiguousarray(f("pool_w")[0].reshape(D, 512), np.float32),
        "pool_sc_l": _lay(f("pool_scale")[0], KC),
        "ffn_w_in": np.ascontiguousarray(f("ffn_w_in"), np.float32),
        "ffn_w_out": np.ascontiguousarray(f("ffn_w_out"), np.float32),
        "inv_freq": np.power(np.float32(10000.0), -np.linspace(0.0, 1.0, 128, dtype=np.float32)).astype(np.float32).reshape(128, 1),
        "ident": np.eye(128, dtype=np.float32),
        "m01T": np.triu(np.ones((128, 128), np.float32)),
        "cpos": np.arange(1, 129, dtype=np.float32).reshape(1, 128),
        "invcnt": np.stack([1.0 / np.minimum(np.arange(1, S + 1), w) for w in (2, 4, 8, 16)]).astype(np.float32),
    }
    x, c, pos = f("x"), f("c"), f("positions")
    maps = []
    for b in cores:
        m = dict(shared)
        m["xT"] = np.ascontiguousarray(x[b].T, np.float32)
        m["c_l"] = _lay(c[b], KC)
        m["pos"] = np.ascontiguousarray(pos[b].reshape(1, S), np.int32)
        maps.append(m)
    return maps


_NC_CACHE = {}


def kernel(**inputs):
    n = 8
    if "full" not in _NC_CACHE:
        _NC_CACHE["full"] = build_program()
    nc = _NC_CACHE["full"]
    in_maps = make_in_maps(inputs, list(range(n)))
    res = run_bass_kernel_spmd(nc, in_maps, core_ids=list(range(n)))
    out = np.stack([np.ascontiguousarray(res.results[b]["outT"].T) for b in range(n)], axis=0)
    return out.astype(np.float32)
```

```python
import math
import os
from contextlib import ExitStack

import numpy as np
import concourse.bass as bass
import concourse.mybir as mybir
from concourse.bass_utils import run_bass_kernel_spmd

F32 = mybir.dt.float32
BF16 = mybir.dt.bfloat16
I32 = mybir.dt.int32
AF = mybir.ActivationFunctionType
ALU = mybir.AluOpType
ENGS = ["sync", "scalar", "vector", "gpsimd", "tensor"]
SEM_LIMIT = 16000
SAME_ENGINE_SYNC = {"scalar": True, "vector": True, "gpsimd": True, "tensor": False, "sync": False}

D = 2048
S = 2048
L = 4
KC = 16
DFF = 5632
JC = 44
EPS = 1e-6
PI = math.pi
NSLOT = 3
SLOT = 8192


class Sem:
    __slots__ = ("h", "n", "name", "eng")

    def __init__(self, h, name, eng=None):
        self.h, self.n, self.name, self.eng = h, 0, name, eng


class Prog:
    def __init__(self, nc, stack):
        self.nc, self.stack = nc, stack
        self.q = {e: [] for e in ENGS}
        self.nsem = 0
        self.prog = {e: None for e in ENGS}
        self.last_w = {}
        self.readers = {}
        self.pending = {e: [] for e in ENGS}

    def sem(self, name, eng=None):
        self.nsem += 1
        return Sem(self.stack.enter_context(self.nc.semaphore(f"{name}_{self.nsem}")), name, eng)

    def sbuf(self, name, shape, dt):
        return self.stack.enter_context(self.nc.sbuf_tensor(name, list(shape), dt))

    def psum(self, name, shape, dt=F32):
        return self.stack.enter_context(self.nc.psum_tensor(name, list(shape), dt))

    def _deps(self, eng, reads, writes, extra):
        w = {}

        def add(tok):
            if tok is None:
                return
            s, v = tok
            if s.eng == eng and not SAME_ENGINE_SYNC[eng]:
                return
            if w.get(id(s), (s, 0))[1] < v:
                w[id(s)] = (s, v)

        for k in reads:
            add(self.last_w.get(k))
        for k in writes:
            add(self.last_w.get(k))
            for t in self.readers.get(k, {}).values():
                add(t)
        for t in extra:
            add(t)
        return list(w.values())

    def _register(self, tok, reads, writes):
        s, v = tok
        for k in reads:
            self.readers.setdefault(k, {})[id(s)] = tok
        for k in writes:
            self.last_w[k] = tok
            self.readers[k] = {}

    def op(self, eng, fn, reads=(), writes=(), mark=True, extra=()):
        waits = self._deps(eng, reads, writes, extra)
        tok = None
        sem = None
        if mark:
            sem = self.prog[eng]
            if sem is None or sem.n >= SEM_LIMIT:
                sem = self.prog[eng] = self.sem("pg_" + eng, eng)
            sem.n += 1
            tok = (sem, sem.n)
            for r, wr in self.pending[eng]:
                self._register(tok, r, wr)
            self.pending[eng] = []
            self._register(tok, reads, writes)
        else:
            self.pending[eng].append((tuple(reads), tuple(writes)))
        self.q[eng].append((fn, waits, sem, 1))
        return tok

    def dma(self, eng, out, in_, sem, reads=(), writes=(), extra=()):
        waits = self._deps(eng, reads, writes, extra)
        sem.n += 16
        tok = (sem, sem.n)
        self._register(tok, reads, writes)
        self.q[eng].append((lambda e: e.dma_start(out=out, in_=in_), waits, sem, 16))
        return tok

    def wait_all(self, eng, toks):
        self.q[eng].append((None, self._deps(eng, (), (), toks), None, 0))

    def run(self):
        for e in ENGS:
            assert not self.pending[e], f"unmarked trailing ops on {e}"
        with self.nc.Block() as block:
            for eng in ENGS:
                getattr(block, eng)(lambda e, eng=eng: self._replay(e, eng))

    def _replay(self, e, eng):
        waited = {}
        for fn, waits, sem, by in self.q[eng]:
            for s, v in waits:
                if waited.get(id(s), 0) >= v:
                    continue
                e.wait_ge(s.h, v)
                waited[id(s)] = v
            if fn is None:
                continue
            ins = fn(e)
            if sem is not None:
                ins.then_inc(sem.h, by)


class RR:
    def __init__(self, items):
        self.items, self.i = items, 0

    def next(self):
        it = self.items[self.i % len(self.items)]
        self.i += 1
        return it


def build_program(nsub=2 * L, final=True):
    nc = bass.Bass("TRN2", target_bir_lowering=False)

    def din(name, shape, dt=F32):
        return nc.dram_tensor(name, list(shape), dt, kind="ExternalInput").ap()

    xT_d = din("xT", [D, S])
    c_d = din("c_l", [128, KC])
    pos_d = din("pos", [1, S], I32)
    wmod_d = din("w_mod", [L, D, 6 * D])
    bmod_d = din("b_mod_l", [128, L * 96])
    n1g_d = din("n1g_l", [128, L * KC])
    n2g_d = din("n2g_l", [128, L * KC])
    fing_d = din("fing_l", [128, KC])
    ret_win_d = din("ret_w_in", [D, 12288])
    ret_wout_d = din("ret_w_out", [4096, D])
    conv_win_d = din("conv_w_in", [D, 3 * D])
    convw_d = din("conv_w_l", [128, 3 * KC])
    conv_wout_d = din("conv_w_out", [D, D])
    gla_win_d = din("gla_w_in", [D, 6160])
    gla_wup_d = din("gla_w_up", [16, 1024])
    gla_negb_d = din("gla_negb_l", [128, 8])
    gla_wout_d = din("gla_w_out", [D, D])
    pool_w_d = din("pool_w", [D, 512])
    pool_sc_d = din("pool_sc_l", [128, KC])
    ffn_win_d = din("ffn_w_in", [L, D, 2 * DFF])
    ffn_wout_d = din("ffn_w_out", [L, DFF, D])
    invf_d = din("inv_freq", [128, 1])
    ident_d = din("ident", [128, 128])
    m01_d = din("m01T", [128, 128])
    cpos_d = din("cpos", [1, 128])
    invc_d = din("invcnt", [4, S])
    outT_d = nc.dram_tensor("outT", [D, S], F32, kind="ExternalOutput").ap()
    xs_d = nc.dram_tensor("xs_scr", [D, S], F32).ap()
    cs_d = nc.dram_tensor("cs_scr", [2, 128, S], F32).ap()
    st_d = nc.dram_tensor("st_scr", [8, 128, 1024], F32).ap()

    xin_v = xT_d.rearrange("(kc p) t -> p kc t", p=128)
    xs_v = xs_d.rearrange("(kc p) t -> p kc t", p=128)
    out_v = outT_d.rearrange("(kc p) t -> p kc t", p=128)

    RET_LOGG = [math.log1p(-2.0 ** (-5.0 - h)) for h in range(8)]

    with ExitStack() as st:
        P = Prog(nc, st)
        A = P.sbuf("A", [128, KC, 1024], BF16)
        FLEX = P.sbuf("FLEX", [128, JC * 1024], BF16)
        wslot = [P.sbuf(f"wslot{i}", [128, SLOT], BF16) for i in range(NSLOT)]
        wsem = [P.sem(f"wsem{i}") for i in range(NSLOT)]
        xst = RR([(P.sbuf(f"xst{i}", [128, 512], F32), ("xst", i), P.sem(f"xst{i}")) for i in range(3)])
        xe = RR([(P.sbuf(f"xe{i}", [128, 512], F32), ("xe", i), P.sem(f"xel{i}"), P.sem(f"xes{i}")) for i in range(3)])
        sqb = RR([(P.sbuf(f"sqb{i}", [128, 512], BF16), ("sqb", i)) for i in range(4)])
        stf = RR([(P.sbuf(f"stf{i}", [128, 512], F32), ("stf", i)) for i in range(4)])
        rvec = P.sbuf("rvec", [128, S], F32)
        elast = P.sbuf("elast", [128, 2, 4], F32)
        ones_bf = P.sbuf("ones_bf", [128, 128], BF16)
        ident = P.sbuf("ident_sb", [128, 128], F32)
        m01 = P.sbuf("m01_sb", [128, 128], F32)
        cpos = P.sbuf("cpos_sb", [128, 128], F32)
        invf = P.sbuf("invf_sb", [128, 1], F32)
        c32 = P.sbuf("c32", [128, KC], F32)
        cact = P.sbuf("cact", [128, KC], BF16)
        bmod = P.sbuf("bmod", [128, L * 96], F32)
        modsb = P.sbuf("modsb", [128, L * 96], F32)
        n1g = P.sbuf("n1g", [128, L * KC], F32)
        n2g = P.sbuf("n2g", [128, L * KC], F32)
        fing = P.sbuf("fing", [128, KC], F32)
        avec = P.sbuf("avec", [128, 2 * L * KC], F32)
        poolsc = P.sbuf("poolsc", [128, KC], F32)
        g1p = P.sbuf("g1p", [128, KC], F32)
        convw = P.sbuf("convw", [128, 3 * KC], F32)
        negb = P.sbuf("negb", [128, 8], F32)
        uc_carry = P.sbuf("uc_carry", [128, KC, 2], F32)
        pool_carry = P.sbuf("pool_carry", [128, KC, 16], F32)
        ps = [P.psum(f"ps{i}", [128, 512]) for i in range(8)]
        psr = RR([(ps[i], ("ps", i)) for i in range(8)])
        psr_hi = RR([(ps[i], ("ps", i)) for i in range(4, 8)])
        psr_lo = RR([(ps[i], ("ps", i)) for i in range(0, 4)])
        psr6 = RR([(ps[i], ("ps", i)) for i in range(0, 6)])
        nbanks = [(ps[6], ("ps", 6)), (ps[7], ("ps", 7))]
        msem = P.sem("misc")
        msem_tok = [None]

        def mdma(eng, out, in_, reads=(), writes=()):
            msem_tok[0] = P.dma(eng, out, in_, msem, reads=reads, writes=writes, extra=[msem_tok[0]] if msem_tok[0] else [])
            return msem_tok[0]

        def fx(off, n, dt=BF16):
            v = FLEX[:, off:off + n]
            return v.bitcast(F32) if dt == F32 else v

        G_ffn = FLEX[:, :].rearrange("p (j t) -> p j t", j=JC)
        OG = fx(0, 32 * 512).rearrange("p (j t) -> p j t", j=32)
        off = 32 * 512
        off0 = off
        qdec = fx(off, 1024).rearrange("p (c t) -> p c t", c=2); off += 1024
        kdec = fx(off, 1024).rearrange("p (c t) -> p c t", c=2); off += 1024
        kctok = fx(off, 1024).rearrange("p (c d) -> p c d", c=4); off += 1024
        vtok = fx(off, 2048).rearrange("p (c v) -> p c v", c=4); off += 2048
        gsil = fx(off, 2048).rearrange("p (c t) -> p c t", c=4); off += 2048
        Smask = fx(off, 512).rearrange("p (c t) -> p c t", c=4); off += 512
        sqo = fx(off, 2048).rearrange("p (c t) -> p c t", c=4); off += 2048
        Sbf2 = [fx(off + i * 1024, 1024).rearrange("p (c t) -> p c t", c=2) for i in range(2)]; off += 2048
        bA = fx(off, 2048, F32).rearrange("p (c t) -> p c t", c=2); off += 2048
        bB = fx(off, 2048, F32).rearrange("p (c t) -> p c t", c=2); off += 2048
        Ebuf = fx(off, 2048, F32).rearrange("p (c t) -> p c t", c=2); off += 2048
        qrot = fx(off, 2048, F32).rearrange("p (c t) -> p c t", c=2); off += 2048
        krot = fx(off, 2048, F32).rearrange("p (c t) -> p c t", c=2); off += 2048
        S32 = fx(off, 2048, F32).rearrange("p (c t) -> p c t", c=2); off += 2048
        cs_sb = fx(off, 2048, F32).rearrange("p (c t) -> p c t", c=2)
        zT = fx(off, 1024, F32)[0:16, :]; off += 1024
        wup = fx(off, 2048, F32)[0:16, :]; off += 2048
        rhead = fx(off, 1024, F32); off += 1024
        assert off <= JC * 1024, off
        off = off0
        ucx = fx(off, 1056, F32)[:, 0:514]; off += 1056
        hx = RR([(fx(off + i * 1056, 1056, F32), ("hx", i)) for i in range(2)]); off += 2 * 1056
        Wp = [fx(off + i * 1056, 1056, F32) for i in range(4)]; off += 4 * 1056
        invc = fx(off, 4096, F32).rearrange("p (c t) -> p c t", c=4); off += 4096
        assert off <= JC * 1024

        plan = []
        wstate = {"issued": 0, "next": 0}

        def w_issue():
            i = wstate["issued"]
            if i >= len(plan):
                return
            s = i % NSLOT
            for src, dstf in plan[i]:
                P.dma("gpsimd", dstf(wslot[s]), src, wsem[s], writes=[("w", s)])
            wstate["issued"] += 1

        def w_get():
            i = wstate["next"]
            wstate["next"] += 1
            while wstate["issued"] < min(i + NSLOT, len(plan)) and wstate["issued"] <= i:
                w_issue()
            return wslot[i % NSLOT], ("w", i % NSLOT)

        def w_done():
            while wstate["issued"] < min(wstate["next"] - 1 + NSLOT + 1, len(plan)):
                w_issue()

        def view(slot, kcn, fb):
            return slot[:, 0:kcn * fb].rearrange("p (k f) -> p k f", f=fb)

        def blk_cols(w2d, kcn, colranges):
            wv = w2d.rearrange("(kc p) f -> p kc f", p=128)
            fb = sum(b - a for a, b in colranges)
            descs = []
            o = 0
            for a, b in colranges:
                descs.append((wv[:, :, a:b], (lambda slot, o=o, n=b - a: view(slot, kcn, fb)[:, :, o:o + n])))
                o += b - a
            return descs

        def plan_all():
            for b in range(24):
                plan.append(blk_cols(wmod_d[0], KC, [(b * 512, (b + 1) * 512)]))
            pm = {"cnt": 0, "b": 0}

            def plan_maybe_mod(l):
                if l + 1 >= L or pm["b"] >= 24:
                    return
                pm["cnt"] += 1
                if pm["cnt"] % 2 == 0:
                    plan.append(blk_cols(wmod_d[l + 1], KC, [(pm["b"] * 512, (pm["b"] + 1) * 512)]))
                    pm["b"] += 1
            sub = 0
            for l in range(L):
                if sub >= nsub:
                    break
                m = l % 4
                for q in range(4):
                    if m == 0:
                        for hd in range(8):
                            plan.append(blk_cols(ret_win_d, KC, [(hd * 256, hd * 256 + 256), (2048 + hd * 256, 2048 + hd * 256 + 256)]))
                            plan.append(blk_cols(ret_win_d, KC, [(4096 + hd * 512, 4096 + hd * 512 + 512)]))
                            plan.append(blk_cols(ret_win_d, KC, [(8192 + hd * 512, 8192 + hd * 512 + 512)]))
                        for b in range(8):
                            plan.append(blk_cols(ret_wout_d, 32, [(b * 256, b * 256 + 256)]))
                    elif m == 1:
                        for fc in range(16):
                            plan.append(blk_cols(conv_win_d, KC, [(fc * 384, fc * 384 + 384)]))
                        for b in range(4):
                            plan.append(blk_cols(conv_wout_d, KC, [(b * 512, b * 512 + 512)]))
                    elif m == 2:
                        plan.append(blk_cols(gla_win_d, KC, [(6144, 6160)]))
                        for hd in range(4):
                            plan.append(blk_cols(gla_win_d, KC, [(hd * 256, hd * 256 + 256), (1024 + hd * 256, 1024 + hd * 256 + 256)]))
                            plan.append(blk_cols(gla_win_d, KC, [(2048 + hd * 512, 2048 + hd * 512 + 512)]))
                            plan.append(blk_cols(gla_win_d, KC, [(4096 + hd * 512, 4096 + hd * 512 + 512)]))
                        for b in range(4):
                            plan.append(blk_cols(gla_wout_d, KC, [(b * 512, b * 512 + 512)]))
                    else:
                        for g in range(4):
                            plan.append(blk_cols(pool_w_d[g * 512:(g + 1) * 512, :], 4, [(0, 512)]))
                sub += 1
                if sub >= nsub:
                    break
                pm["cnt"], pm["b"] = 0, 0
                for hf in range(2):
                    for b in range(22):
                        plan.append(blk_cols(ffn_win_d[l], KC, [(b * 256, b * 256 + 256), (DFF + b * 256, DFF + b * 256 + 256)]))
                        plan_maybe_mod(l)
                    for i in range(16):
                        plan.append(blk_cols(ffn_wout_d[l], JC, [(i * 128, i * 128 + 128)]))
                        plan_maybe_mod(l)
                sub += 1

        plan_all()

        def act(fn, reads=(), writes=()):
            return P.op("scalar", fn, reads=reads, writes=writes)

        def dve(fn, reads=(), writes=()):
            return P.op("vector", fn, reads=reads, writes=writes)

        def pe(fn, reads=(), writes=(), mark=False):
            return P.op("tensor", fn, reads=reads, writes=writes, mark=mark)

        def rsqrt_from_psum(pb, pk, dst, dkey, scale):
            act(lambda e: e.activation(out=dst, in_=pb[:], func=AF.Ln, bias=EPS, scale=scale), reads=[pk], writes=[dkey])
            act(lambda e: e.activation(out=dst, in_=dst, func=AF.Exp, scale=-0.5), reads=[dkey], writes=[dkey])

        def rv(tbg):
            return rvec[:, tbg * 512:(tbg + 1) * 512]

        def norm_sums(t0, tt, src):
            for tb in range(tt // 512):
                pb, pk = psr.next()
                ta = t0 + tb * 512
                for kc in range(KC):
                    xb, xk, xsem = xst.next()
                    P.dma("sync", xb[:], src[:, kc, ta:ta + 512], xsem, reads=[("x", kc, ta // 512)], writes=[xk])
                    sb, sk = sqb.next()
                    act(lambda e, sb=sb, xb=xb: e.activation(out=sb[:], in_=xb[:], func=AF.Square), reads=[xk], writes=[sk])
                    pe(lambda e, pb=pb, sb=sb, kc=kc: e.matmul(pb[:], ones_bf[:], sb[:], start=(kc == 0), stop=(kc == KC - 1)),
                       reads=[sk, "ones"], writes=[pk], mark=True)
                rsqrt_from_psum(pb, pk, rv(ta // 512), ("rvec", ta // 512), 1.0 / D)

        def prologue(t0, tt, a_ap, s_ap, src):
            for tb in range(tt // 512):
                ta = t0 + tb * 512
                tbg = ta // 512
                sl = slice(tb * 512, (tb + 1) * 512)
                for kc in range(KC):
                    xb, xk, xsem = xst.next()
                    P.dma("sync", xb[:], src[:, kc, ta:ta + 512], xsem, reads=[("x", kc, tbg)], writes=[xk])
                    dve(lambda e, xb=xb, kc=kc, tbg=tbg: e.scalar_tensor_tensor(out=xb[:], in0=xb[:], scalar=a_ap[:, kc:kc + 1], in1=rv(tbg),
                                                                                op0=ALU.mult, op1=ALU.mult), reads=[xk, "vecs", ("rvec", tbg)], writes=[xk])
                    act(lambda e, xb=xb, kc=kc, sl=sl: e.activation(out=A[:, kc, sl], in_=xb[:], func=AF.Identity, bias=s_ap[:, kc:kc + 1], scale=1.0),
                        reads=[xk, "vecs"], writes=[("h", kc, tb)])
                    yield

        class Stepper:
            def __init__(self, gen, total):
                self.gen, self.total, self.started = gen, total, False

            def start(self):
                self.started = True

            def step(self, ngroups):
                if not self.started or self.gen is None:
                    return
                for _ in range(-(-self.total // ngroups)):
                    if next(self.gen, "end") == "end":
                        self.gen = None
                        return

            def finish(self):
                if self.gen is not None:
                    for _ in self.gen:
                        pass
                    self.gen = None

        def gemm_fm(pb, pk, wv, wk, fsl, actv, akeys, kcn, tsl, mark=True):
            for kc in range(kcn):
                pe(lambda e, kc=kc: e.matmul(pb, wv[:, kc, fsl], actv[:, kc, tsl], start=(kc == 0), stop=(kc == kcn - 1)),
                   reads=[wk, akeys[kc]], writes=[pk], mark=(mark and kc == kcn - 1))

        def x_load(fc, tbg, src):
            eb, ek, lsem, ssem = xe.next()
            P.dma("sync", eb[:], src[:, fc, tbg * 512:(tbg + 1) * 512], lsem, reads=[("x", fc, tbg)], writes=[ek])
            return eb, ek, ssem

        def x_epilogue(pb, pk, gate_col, fc, tbg, loaded, nb):
            eb, ek, ssem = loaded
            dve(lambda e: e.scalar_tensor_tensor(out=eb[:], in0=pb[:], scalar=gate_col, in1=eb[:], op0=ALU.mult, op1=ALU.add),
                reads=[pk, ek, "vecs"], writes=[ek])
            P.dma("sync", xs_v[:, fc, tbg * 512:(tbg + 1) * 512], eb[:], ssem, reads=[ek], writes=[("x", fc, tbg)])
            sb, sk = sqb.next()
            act(lambda e: e.activation(out=sb[:], in_=eb[:], func=AF.Square), reads=[ek], writes=[sk])
            nbp, nbk = nb

            def deferred():
                pe(lambda e: e.matmul(nbp[:], ones_bf[:], sb[:], start=(fc == 0), stop=(fc == KC - 1)), reads=[sk, "ones"], writes=[nbk], mark=True)
                if fc == KC - 1:
                    rsqrt_from_psum(nbp, nbk, rv(tbg), ("rvec", tbg), 1.0 / D)
            return deferred

        def outproj(nblk, kcn, fb, actv, akeys, gate_ap, t0, tt, src, fc_of=None, after_block=None, stepper=None):
            ntb = tt // 512
            nfc = fb // 128
            ngroups = nblk * nfc * ntb
            glist = [(b * nfc + f if fc_of is None else fc_of(b, f), t0 // 512 + tb) for b in range(nblk) for f in range(nfc) for tb in range(ntb)]
            loads = []

            def ensure_loaded(upto):
                while len(loads) <= min(upto, ngroups - 1):
                    fc_, tbg_ = glist[len(loads)]
                    loads.append(x_load(fc_, tbg_, src))

            defq = []
            gi = 0
            for b in range(nblk):
                slot, wk = w_get()
                wv = view(slot, kcn, fb)
                for f in range(nfc):
                    fc = b * nfc + f if fc_of is None else fc_of(b, f)
                    for tb in range(ntb):
                        ensure_loaded(gi + 1)
                        pb, pk = psr6.next()
                        a_v, a_k = (actv, akeys) if fc_of is None else actv(b)
                        gemm_fm(pb[:], pk, wv, wk, slice(f * 128, (f + 1) * 128), a_v, a_k, kcn, slice(tb * 512, (tb + 1) * 512))
                        defq.append(x_epilogue(pb, pk, gate_ap[:, fc:fc + 1], fc, t0 // 512 + tb, loads[gi], nbanks[tb]))
                        gi += 1
                        if len(defq) > 2:
                            defq.pop(0)()
                        if stepper is not None:
                            stepper.step(ngroups)
                w_done()
                if after_block is not None:
                    after_block()
            for d_ in defq:
                d_()

        for dst, src, key in [(ident, ident_d, "ident"), (m01, m01_d, "m01"), (invf, invf_d, "invf"), (c32, c_d, "c32"), (bmod, bmod_d, "bmod"),
                              (n1g, n1g_d, "n1g"), (n2g, n2g_d, "n2g"), (fing, fing_d, "fing"), (poolsc, pool_sc_d, "poolsc"),
                              (convw, convw_d, "vecs0"), (negb, gla_negb_d, "negb")]:
            mdma("sync", dst[:], src, writes=[key])
        mdma("sync", cpos[:], cpos_d.partition_broadcast(128), writes=["cpos"])
        dve(lambda e: e.memset(ones_bf[:], 1.0), writes=["ones"])
        dve(lambda e: e.memset(uc_carry[:], 0.0), writes=["uc_carry"])
        dve(lambda e: e.memset(pool_carry[:], 0.0), writes=["pool_carry"])
        act(lambda e: e.activation(out=cact[:], in_=c32[:], func=AF.Silu), reads=["c32"], writes=["cact"])
        HI = 6.28125
        LO = 2 * PI - HI
        posi = xst.items[0][0].bitcast(I32)
        angb = xst.items[1][0]
        an = xst.items[2][0]
        nI = stf.items[0][0].bitcast(I32)
        nF = stf.items[1][0]
        rr_ = stf.items[2][0]
        kposi, kang, kan, knI, knF, krr = ("xst", 0), ("xst", 1), ("xst", 2), ("stf", 0), ("stf", 1), ("stf", 2)
        for blk in range(S // 512):
            tsl = slice(blk * 512, (blk + 1) * 512)
            mdma("sync", posi[:], pos_d[:, tsl].partition_broadcast(128), writes=[kposi])
            dve(lambda e: e.tensor_copy(out=angb[:], in_=posi[:]), reads=[kposi], writes=[kang])
            dve(lambda e: e.tensor_scalar(out=angb[:], in0=angb[:], scalar1=invf[:, 0:1], scalar2=None, op0=ALU.mult), reads=[kang, "invf"], writes=[kang])
            for which, shift in ((0, 0.5 * PI), (1, 0.0)):
                dve(lambda e, shift=shift: e.tensor_scalar(out=an[:], in0=angb[:], scalar1=shift, scalar2=None, op0=ALU.add), reads=[kang], writes=[kan])
                dve(lambda e: e.tensor_scalar(out=nF[:], in0=an[:], scalar1=1.0 / (2 * PI), scalar2=None, op0=ALU.mult), reads=[kan], writes=[knF])
                dve(lambda e: e.tensor_copy(out=nI[:], in_=nF[:]), reads=[knF], writes=[knI])
                dve(lambda e: e.tensor_copy(out=nF[:], in_=nI[:]), reads=[knI], writes=[knF])
                dve(lambda e: e.scalar_tensor_tensor(out=rr_[:], in0=nF[:], scalar=-HI, in1=an[:], op0=ALU.mult, op1=ALU.add), reads=[knF, kan], writes=[krr])
                dve(lambda e: e.scalar_tensor_tensor(out=rr_[:], in0=nF[:], scalar=-LO, in1=rr_[:], op0=ALU.mult, op1=ALU.add), reads=[knF, krr], writes=[krr])
                dve(lambda e: e.tensor_scalar(out=nF[:], in0=rr_[:], scalar1=PI, scalar2=2 * PI, op0=ALU.is_gt, op1=ALU.mult), reads=[krr], writes=[knF])
                dve(lambda e: e.tensor_tensor(out=rr_[:], in0=rr_[:], in1=nF[:], op=ALU.subtract), reads=[knF, krr], writes=[krr])
                dve(lambda e: e.tensor_scalar(out=nF[:], in0=rr_[:], scalar1=-PI, scalar2=2 * PI, op0=ALU.is_lt, op1=ALU.mult), reads=[krr], writes=[knF])
                dve(lambda e: e.tensor_tensor(out=rr_[:], in0=rr_[:], in1=nF[:], op=ALU.add), reads=[knF, krr], writes=[krr])
                act(lambda e: e.activation(out=rr_[:], in_=rr_[:], func=AF.Sin), reads=[krr], writes=[krr])
                mdma("sync", cs_d[which, :, tsl], rr_[:], reads=[krr], writes=[("cs", which)])
        def mod(l, j):
            return modsb[:, l * 96 + j * 16: l * 96 + (j + 1) * 16]

        def mod_block(l, b):
            slot, wk = w_get()
            wv = view(slot, KC, 512)
            pb, pk = psr6.next()
            for f in range(4):
                for kc in range(KC):
                    pe(lambda e, f=f, kc=kc: e.matmul(pb[:, f:f + 1], wv[:, kc, f * 128:(f + 1) * 128], cact[:, kc:kc + 1], start=(kc == 0), stop=(kc == KC - 1)),
                       reads=[wk, "cact"], writes=[pk], mark=(f == 3 and kc == KC - 1))
            w_done()
            c0 = l * 96 + b * 4
            dve(lambda e: e.tensor_tensor(out=modsb[:, c0:c0 + 4], in0=pb[:, 0:4], in1=bmod[:, c0:c0 + 4], op=ALU.add), reads=[pk, "bmod"], writes=["modsb"])

        def mod_finish(l):
            for which, (gsrc, jsc) in enumerate(((n1g, 1), (n2g, 4))):
                dst = avec[:, (l * 2 + which) * KC:(l * 2 + which + 1) * KC]
                dve(lambda e, dst=dst, jsc=jsc: e.tensor_scalar(out=dst, in0=mod(l, jsc), scalar1=1.0, scalar2=None, op0=ALU.add), reads=["modsb"], writes=["vecs"])
                dve(lambda e, dst=dst, gsrc=gsrc: e.tensor_tensor(out=dst, in0=dst, in1=gsrc[:, l * KC:(l + 1) * KC], op=ALU.mult), reads=["vecs", "n1g", "n2g"], writes=["vecs"])
            if l == 3:
                dve(lambda e: e.tensor_tensor(out=g1p[:], in0=mod(3, 2), in1=poolsc[:], op=ALU.mult), reads=["modsb", "poolsc"], writes=["vecs"])

        norm_sums(0, S, xin_v)
        for b in range(24):
            mod_block(0, b)
        mod_finish(0)
        dve(lambda e: e.tensor_copy(out=convw[:], in_=convw[:]), reads=["vecs0"], writes=["vecs"])
        modst = {"cnt": 0, "b": 0}

        def maybe_mod(l):
            if l + 1 >= L or modst["b"] >= 24:
                return
            modst["cnt"] += 1
            if modst["cnt"] % 2 == 0:
                mod_block(l + 1, modst["b"])
                modst["b"] += 1
                if modst["b"] == 24:
                    mod_finish(l + 1)

        def mixer_pro(l, t0, src):
            return Stepper(prologue(t0, 512, avec[:, (l * 2) * KC:(l * 2 + 1) * KC], mod(l, 0), src), 16)

        def ffn_pro(l, hf):
            return Stepper(prologue(hf * 1024, 1024, avec[:, (l * 2 + 1) * KC:(l * 2 + 2) * KC], mod(l, 3), xs_v), 32)

        def lin_attn(l, kind, q_, t0, src, hook):
            TT = 512
            nh = 8 if kind == "ret" else 4
            hk = [("h", kc, 0) for kc in range(KC)]
            bA4 = bA.rearrange("p c (n t) -> p c n t", n=4)
            bB4 = bB.rearrange("p c (n t) -> p c n t", n=4)
            E4 = Ebuf.rearrange("p c (n t) -> p c n t", n=4)
            krot4 = krot.rearrange("p c (n t) -> p c n t", n=4)
            if kind == "ret":
                mdma("sync", cs_sb[:, 0, :], cs_d[0, :, t0:t0 + TT], reads=[("cs", 0)], writes=["cs_sb"])
                mdma("sync", cs_sb[:, 1, :], cs_d[1, :, t0:t0 + TT], reads=[("cs", 1)], writes=["cs_sb"])
            else:
                mdma("sync", wup, gla_wup_d, writes=["wup"])
                slot, wk = w_get()
                wv = view(slot, KC, 16)
                pb, pk = psr.next()
                gemm_fm(pb[0:16, :], pk, wv, wk, slice(0, 16), A, hk, KC, slice(0, TT))
                w_done()
                act(lambda e, pb=pb: e.activation(out=zT, in_=pb[0:16, :], func=AF.Copy), reads=[pk], writes=["zT"])
            for hd in range(nh):
                if kind == "ret":
                    dve(lambda e, hd=hd: e.tensor_scalar(out=bA4, in0=cpos[:, :].unsqueeze(1).unsqueeze(1).to_broadcast([128, 2, 4, 128]), scalar1=RET_LOGG[hd],
                                                         scalar2=None, op0=ALU.mult), reads=["cpos"], writes=["bA"])
                    bfin, bkey = bA, "bA"
                else:
                    for dc in range(2):
                        pb, pk = psr.next()
                        pe(lambda e, pb=pb, hd=hd, dc=dc: e.matmul(pb[:], wup[:, (hd * 2 + dc) * 128:(hd * 2 + dc + 1) * 128], zT, start=True, stop=True),
                           reads=["wup", "zT"], writes=[pk], mark=True)
                        act(lambda e, pb=pb, hd=hd, dc=dc: e.activation(out=bA[:, dc, :], in_=pb[:], func=AF.Exp, bias=negb[:, hd * 2 + dc:hd * 2 + dc + 1], scale=-1.0),
                            reads=[pk, "negb"], writes=["bA"])
                    act(lambda e: e.activation(out=bA, in_=bA, func=AF.Ln, bias=1.0, scale=1.0), reads=["bA"], writes=["bA"])
                    dve(lambda e: e.tensor_scalar(out=bA, in0=bA, scalar1=-1.0 / 16.0, scalar2=None, op0=ALU.mult), reads=["bA"], writes=["bA"])
                    src4, dst4, sk_, dk_ = bA4, bB4, "bA", "bB"
                    for sh in (1, 2, 4, 8, 16, 32, 64):
                        dve(lambda e, src4=src4, dst4=dst4, sh=sh: e.tensor_tensor(out=dst4[:, :, :, sh:], in0=src4[:, :, :, sh:], in1=src4[:, :, :, :128 - sh], op=ALU.add),
                            reads=[sk_], writes=[dk_])
                        act(lambda e, src4=src4, dst4=dst4, sh=sh: e.activation(out=dst4[:, :, :, :sh], in_=src4[:, :, :, :sh], func=AF.Copy), reads=[sk_], writes=[dk_])
                        src4, dst4, sk_, dk_ = dst4, src4, dk_, sk_
                    bfin, bkey = bB, "bB"
                slot, wk = w_get()
                wv = view(slot, KC, 512)
                banks = [psr_hi.next() for _ in range(4)]
                for f, (pb, pk) in enumerate(banks):
                    gemm_fm(pb[:], pk, wv, wk, slice(f * 128, (f + 1) * 128), A, hk, KC, slice(0, TT))
                w_done()
                if kind == "ret":
                    for (b0, b1), dst, dkey, sc in (((banks[0], banks[1]), qrot, "qrot", 1.0), ((banks[2], banks[3]), krot, "krot", 1.0 / 16.0)):
                        t1, k1 = stf.next()
                        t2, k2 = stf.next()
                        m1, km1 = stf.next()
                        m2, km2 = stf.next()
                        act(lambda e, t1=t1, b0=b0, sc=sc: e.activation(out=t1[:], in_=b0[0][:], func=AF.Copy, scale=sc), reads=[b0[1]], writes=[k1])
                        act(lambda e, t2=t2, b1=b1, sc=sc: e.activation(out=t2[:], in_=b1[0][:], func=AF.Copy, scale=sc), reads=[b1[1]], writes=[k2])
                        dve(lambda e, t1=t1, m1=m1: e.tensor_tensor(out=m1[:], in0=t1[:], in1=cs_sb[:, 0, :], op=ALU.mult), reads=[k1, "cs_sb"], writes=[km1])
                        dve(lambda e, t2=t2, m2=m2: e.tensor_tensor(out=m2[:], in0=t2[:], in1=cs_sb[:, 1, :], op=ALU.mult), reads=[k2, "cs_sb"], writes=[km2])
                        dve(lambda e, m1=m1, m2=m2, dst=dst: e.tensor_tensor(out=dst[:, 0, :], in0=m1[:], in1=m2[:], op=ALU.subtract), reads=[km1, km2], writes=[dkey])
                        dve(lambda e, t2=t2, m1=m1: e.tensor_tensor(out=m1[:], in0=t2[:], in1=cs_sb[:, 0, :], op=ALU.mult), reads=[k2, "cs_sb"], writes=[km1])
                        dve(lambda e, t1=t1, m2=m2: e.tensor_tensor(out=m2[:], in0=t1[:], in1=cs_sb[:, 1, :], op=ALU.mult), reads=[k1, "cs_sb"], writes=[km2])
                        dve(lambda e, m1=m1, m2=m2, dst=dst: e.tensor_tensor(out=dst[:, 1, :], in0=m1[:], in1=m2[:], op=ALU.add), reads=[km1, km2], writes=[dkey])
                else:
                    for f, (dst, dkey, sc) in enumerate(((qrot, "qrot", 1.0 / 16.0), (qrot, "qrot", 1.0 / 16.0), (krot, "krot", 1.0), (krot, "krot", 1.0))):
                        pb, pk = banks[f]
                        act(lambda e, pb=pb, dst=dst, f=f, sc=sc: e.activation(out=dst[:, f % 2, :], in_=pb[:], func=AF.Copy, scale=sc), reads=[pk], writes=[dkey])
                act(lambda e, bfin=bfin: e.activation(out=Ebuf, in_=bfin, func=AF.Exp), reads=[bkey], writes=["E"])
                dve(lambda e: e.tensor_tensor(out=qdec, in0=qrot, in1=Ebuf, op=ALU.mult), reads=["qrot", "E"], writes=["qdec"])
                dve(lambda e: e.tensor_copy(out=elast[:, :, :].unsqueeze(3), in_=E4[:, :, :, 127:128]), reads=["E"], writes=["elast"])
                act(lambda e, bfin=bfin: e.activation(out=Ebuf, in_=bfin, func=AF.Exp, scale=-1.0), reads=[bkey], writes=["E"])
                dve(lambda e: e.tensor_tensor(out=krot, in0=krot, in1=Ebuf, op=ALU.mult), reads=["krot", "E"], writes=["krot"])
                act(lambda e: e.activation(out=kdec, in_=krot, func=AF.Copy), reads=["krot"], writes=["kdec"])
                dve(lambda e: e.tensor_tensor(out=krot4, in0=krot4, in1=elast[:, :, :].unsqueeze(3).to_broadcast([128, 2, 4, 128]), op=ALU.mult),
                    reads=["krot", "elast"], writes=["krot"])
                slot, wk = w_get()
                wv = view(slot, KC, 512)
                for tt_ in range(4):
                    pb, pk = psr_lo.next()
                    for kc in range(KC):
                        pe(lambda e, pb=pb, wv=wv, kc=kc, tt_=tt_: e.matmul(pb[:], A[:, kc, tt_ * 128:(tt_ + 1) * 128], wv[:, kc, :], start=(kc == 0), stop=(kc == KC - 1)),
                           reads=[wk, ("h", kc, 0)], writes=[pk], mark=(kc == KC - 1))
                    if tt_ % 2 == 0:
                        act(lambda e, pb=pb, tt_=tt_: e.activation(out=vtok[:, tt_, :], in_=pb[:], func=AF.Copy), reads=[pk], writes=["vtok"])
                    else:
                        dve(lambda e, pb=pb, tt_=tt_: e.tensor_copy(out=vtok[:, tt_, :], in_=pb[:]), reads=[pk], writes=["vtok"])
                w_done()
                slot, wk = w_get()
                wv = view(slot, KC, 512)
                for f in range(4):
                    pb, pk = psr_lo.next()
                    gemm_fm(pb[:], pk, wv, wk, slice(f * 128, (f + 1) * 128), A, hk, KC, slice(0, TT))
                    act(lambda e, pb=pb, f=f: e.activation(out=gsil[:, f, :], in_=pb[:], func=AF.Silu), reads=[pk], writes=["gsil"])
                w_done()
                if hd == nh - 1:
                    hook.start()
                for c2 in range(2):
                    pb, pk = psr_hi.next()
                    for cc in range(2):
                        c = c2 * 2 + cc
                        for dc in range(2):
                            pe(lambda e, pb=pb, cc=cc, dc=dc, c=c: e.transpose(pb[:, cc * 256 + dc * 128: cc * 256 + (dc + 1) * 128], krot[:, dc, c * 128:(c + 1) * 128], ident[:]),
                               reads=["krot", "ident"], writes=[pk], mark=(cc == 1 and dc == 1))
                    act(lambda e, pb=pb, c2=c2: e.activation(out=kctok[:, c2 * 2:c2 * 2 + 2, :], in_=pb[:].rearrange("p (c d) -> p c d", c=2), func=AF.Copy),
                        reads=[pk], writes=["kctok"])
                if q_ == 0:
                    dve(lambda e: e.memset(S32, 0.0), writes=["S32"])
                else:
                    mdma("sync", S32, st_d[hd].rearrange("p (c v) -> p c v", c=2), reads=[("st", hd)], writes=["S32"])
                act(lambda e: e.activation(out=Sbf2[0], in_=S32, func=AF.Copy), reads=["S32"], writes=[("Sbf", 0)])
                pb, pk = psr_hi.next()
                for c in range(4):
                    csl = slice(c * 128, (c + 1) * 128)
                    for dc in range(2):
                        pe(lambda e, pb=pb, csl=csl, dc=dc: e.matmul(pb[:, csl], kdec[:, dc, csl], qdec[:, dc, csl], start=(dc == 0), stop=(dc == 1)),
                           reads=["kdec", "qdec"], writes=[pk], mark=(c == 3 and dc == 1))
                dve(lambda e, pb=pb: e.tensor_tensor(out=Smask, in0=pb[:].rearrange("p (c t) -> p c t", c=4), in1=m01[:, :].unsqueeze(1).to_broadcast([128, 4, 128]), op=ALU.mult),
                    reads=[pk, "m01"], writes=["Smask"])
                okeys = [("ps", vc) for vc in range(4)]
                for c in range(4):
                    csl = slice(c * 128, (c + 1) * 128)
                    Scur, kcur = Sbf2[c % 2], ("Sbf", c % 2)
                    Snxt, knxt = Sbf2[(c + 1) % 2], ("Sbf", (c + 1) % 2)
                    kvb = []
                    for dc in range(2):
                        pb, pk = psr_hi.next()
                        kvb.append((pb, pk))
                        pe(lambda e, pb=pb, c=c, dc=dc: e.matmul(pb[:], kctok[:, c, dc * 128:(dc + 1) * 128], vtok[:, c, :], start=True, stop=True),
                           reads=["kctok", "vtok"], writes=[pk], mark=True)
                    for vc in range(4):
                        vsl = slice(vc * 128, (vc + 1) * 128)
                        pe(lambda e, vc=vc, c=c, csl=csl, vsl=vsl: e.matmul(ps[vc][:, csl], vtok[:, c, vsl], Smask[:, c, :], start=True, stop=False),
                           reads=["vtok", "Smask"], writes=[okeys[vc]])
                        pe(lambda e, vc=vc, csl=csl, vsl=vsl, Scur=Scur: e.matmul(ps[vc][:, csl], Scur[:, 0, vsl], qdec[:, 0, csl], start=False, stop=False),
                           reads=[kcur, "qdec"], writes=[okeys[vc]])
                        pe(lambda e, vc=vc, csl=csl, vsl=vsl, Scur=Scur: e.matmul(ps[vc][:, csl], Scur[:, 1, vsl], qdec[:, 1, csl], start=False, stop=True),
                           reads=[kcur, "qdec"], writes=[okeys[vc]], mark=(vc == 3))
                    for dc in range(2):
                        pb, pk = kvb[dc]
                        dve(lambda e, pb=pb, c=c, dc=dc: e.scalar_tensor_tensor(out=S32[:, dc, :], in0=S32[:, dc, :], scalar=elast[:, dc, c:c + 1], in1=pb[:],
                                                                                op0=ALU.mult, op1=ALU.add), reads=[pk, "S32", "elast"], writes=["S32"])
                    if c < 3:
                        act(lambda e, Snxt=Snxt: e.activation(out=Snxt, in_=S32, func=AF.Copy), reads=["S32"], writes=[knxt])
                mdma("sync", st_d[hd].rearrange("p (c v) -> p c v", c=2), S32, reads=["S32"], writes=[("st", hd)])
                for vc in range(4):
                    act(lambda e, vc=vc: e.activation(out=sqo[:, vc, :], in_=ps[vc][:], func=AF.Square), reads=[okeys[vc]], writes=["sqo"])
                pb, pk = psr_hi.next()
                for vc in range(4):
                    pe(lambda e, pb=pb, vc=vc: e.matmul(pb[:], ones_bf[:], sqo[:, vc, :], start=(vc == 0), stop=(vc == 3)), reads=["sqo", "ones"], writes=[pk], mark=(vc == 3))
                rsqrt_from_psum(pb, pk, rhead, "rhead", 1.0 / 512.0)
                for vc in range(4):
                    tb_, ktb = stf.next()
                    dve(lambda e, vc=vc, tb_=tb_: e.tensor_tensor(out=tb_[:], in0=ps[vc][:], in1=rhead, op=ALU.mult), reads=[okeys[vc], "rhead"], writes=[ktb])
                    dve(lambda e, vc=vc, hd=hd, tb_=tb_: e.tensor_tensor(out=OG[:, hd * 4 + vc, :], in0=tb_[:], in1=gsil[:, vc, :], op=ALU.mult),
                        reads=[ktb, "gsil"], writes=[("og", hd * 4 + vc)])
            kcn = nh * 4
            outproj(8 if kind == "ret" else 4, kcn, SLOT // kcn, OG, [("og", j) for j in range(kcn)], mod(l, 2), t0, TT, src, stepper=hook)

        def conv_mixer(l, q_, t0, src, hook):
            TT = 512
            hk = [("h", kc, 0) for kc in range(KC)]
            YB = OG
            for fc in range(KC):
                slot, wk = w_get()
                wv = view(slot, KC, 384)
                banks = [psr.next() for _ in range(3)]
                for g, (pb, pk) in enumerate(banks):
                    gemm_fm(pb[:], pk, wv, wk, slice(g * 128, (g + 1) * 128), A, hk, KC, slice(0, TT))
                w_done()
                (pbb, pkb), (pbc, pkc), (pbu, pku) = banks
                cs_, kcs = stf.next()
                y_, ky = stf.next()
                act(lambda e, cs_=cs_, pbc=pbc: e.activation(out=cs_[:], in_=pbc[:], func=AF.Copy), reads=[pkc], writes=[kcs])
                act(lambda e, fc=fc: e.activation(out=ucx[:, 0:2], in_=uc_carry[:, fc, :], func=AF.Copy), reads=["uc_carry"], writes=["ucx"])
                dve(lambda e, cs_=cs_, pbu=pbu: e.tensor_tensor(out=ucx[:, 2:514], in0=cs_[:], in1=pbu[:], op=ALU.mult), reads=[kcs, pku, "ucx"], writes=["ucx"])
                act(lambda e, y_=y_, fc=fc: e.activation(out=y_[:], in_=ucx[:, 2:514], func=AF.Copy, scale=convw[:, 2 * KC + fc:2 * KC + fc + 1]), reads=["ucx", "vecs"], writes=[ky])
                dve(lambda e, y_=y_, fc=fc: e.scalar_tensor_tensor(out=y_[:], in0=ucx[:, 1:513], scalar=convw[:, KC + fc:KC + fc + 1], in1=y_[:], op0=ALU.mult, op1=ALU.add),
                    reads=["ucx", ky, "vecs"], writes=[ky])
                dve(lambda e, y_=y_, fc=fc: e.scalar_tensor_tensor(out=y_[:], in0=ucx[:, 0:512], scalar=convw[:, fc:fc + 1], in1=y_[:], op0=ALU.mult, op1=ALU.add),
                    reads=["ucx", ky, "vecs"], writes=[ky])
                dve(lambda e, y_=y_, fc=fc, pbb=pbb: e.tensor_tensor(out=YB[:, fc, :], in0=y_[:], in1=pbb[:], op=ALU.mult), reads=[ky, pkb], writes=[("og", fc)])
                act(lambda e, fc=fc: e.activation(out=uc_carry[:, fc, :], in_=ucx[:, 512:514], func=AF.Copy), reads=["ucx"], writes=["uc_carry"])
            hook.start()
            outproj(4, KC, 512, YB, [("og", j) for j in range(KC)], mod(l, 2), t0, TT, src, stepper=hook)

        def pool_mixer(l, q_, t0, xsrc_ap, hook):
            TT = 512
            a_ap, s_ap = avec[:, (l * 2) * KC:(l * 2 + 1) * KC], mod(l, 0)
            MX = OG
            tbg = t0 // 512
            mdma("sync", invc[:], invc_d[:, t0:t0 + TT].partition_broadcast(128), writes=["invc"])
            n = 16 + TT
            for kp in range(0, KC, 2):
                gi = kp // 4
                ch = []
                for ci, kc in enumerate((kp, kp + 1)):
                    xb, xk, xsem = xst.next()
                    P.dma("sync", xb[:], xsrc_ap[:, kc, t0:t0 + TT], xsem, reads=[("x", kc, tbg)], writes=[xk])
                    hb, hkey = hx.next()
                    ch.append(dict(kc=kc, xb=xb, xk=xk, hb=hb, hkey=hkey, src=hb, skey=hkey, ci=ci))
                for c_ in ch:
                    dve(lambda e, xb=c_["xb"], kc=c_["kc"]: e.scalar_tensor_tensor(out=xb[:], in0=xb[:], scalar=a_ap[:, kc:kc + 1], in1=rv(tbg), op0=ALU.mult, op1=ALU.mult),
                        reads=[c_["xk"], "vecs", ("rvec", tbg)], writes=[c_["xk"]])
                for c_ in ch:
                    act(lambda e, hb=c_["hb"], kc=c_["kc"]: e.activation(out=hb[:, 0:16], in_=pool_carry[:, kc, :], func=AF.Copy), reads=["pool_carry"], writes=[c_["hkey"]])
                    act(lambda e, hb=c_["hb"], xb=c_["xb"], kc=c_["kc"]: e.activation(out=hb[:, 16:16 + TT], in_=xb[:], func=AF.Identity, bias=s_ap[:, kc:kc + 1], scale=1.0),
                        reads=[c_["xk"], "vecs"], writes=[c_["hkey"]])
                    act(lambda e, hb=c_["hb"], kc=c_["kc"]: e.activation(out=pool_carry[:, kc, :], in_=hb[:, TT:TT + 16], func=AF.Copy), reads=[c_["hkey"]], writes=["pool_carry"])
                for si in range(gi + 1):
                    m = 1 << si
                    for c_ in ch:
                        dst, dkey = Wp[c_["ci"] * 2 + si % 2], ("Wp", c_["ci"] * 2 + si % 2)
                        dve(lambda e, src=c_["src"], dst=dst, m=m: e.tensor_tensor(out=dst[:, m:n], in0=src[:, m:n], in1=src[:, 0:n - m], op=ALU.add), reads=[c_["skey"]], writes=[dkey])
                        c_["src"], c_["skey"] = dst, dkey
                for c_ in ch:
                    dve(lambda e, src=c_["src"], gi=gi: e.tensor_tensor(out=src[:, 16:n], in0=src[:, 16:n], in1=invc[:, gi, :], op=ALU.mult), reads=[c_["skey"], "invc"], writes=[c_["skey"]])
                for c_ in ch:
                    dve(lambda e, src=c_["src"], hb=c_["hb"], kc=c_["kc"]: e.tensor_tensor(out=MX[:, kc, :], in0=src[:, 16:n], in1=hb[:, 16:n], op=ALU.subtract),
                        reads=[c_["skey"], c_["hkey"]], writes=[("og", c_["kc"])])
            outproj(4, 4, 512, lambda b: (MX[:, b * 4:(b + 1) * 4, :], [("og", b * 4 + j) for j in range(4)]), None, g1p, t0, TT, xsrc_ap, fc_of=lambda b, f: b * 4 + f)

        def ffn(l, hf, hook):
            TT = 1024
            t0 = hf * TT
            if hf == 0:
                modst["cnt"], modst["b"] = 0, 0
            for b in range(22):
                slot, wk = w_get()
                wv = view(slot, KC, 512)
                for jj in range(2):
                    j = b * 2 + jj
                    for tb in range(2):
                        tsl = slice(tb * 512, (tb + 1) * 512)
                        hk = [("h", kc, tb) for kc in range(KC)]
                        pa, ka = psr.next()
                        pbb, kb = psr.next()
                        gemm_fm(pa[:], ka, wv, wk, slice(jj * 128, (jj + 1) * 128), A, hk, KC, tsl)
                        gemm_fm(pbb[:], kb, wv, wk, slice(256 + jj * 128, 256 + (jj + 1) * 128), A, hk, KC, tsl)
                        sb, sk = stf.next()
                        act(lambda e, sb=sb, pa=pa: e.activation(out=sb[:], in_=pa[:], func=AF.Silu), reads=[ka], writes=[sk])
                        dve(lambda e, sb=sb, pbb=pbb, j=j, tsl=tsl: e.tensor_tensor(out=G_ffn[:, j, tsl], in0=sb[:], in1=pbb[:], op=ALU.mult), reads=[sk, kb], writes=[("g", j)])
                w_done()
                maybe_mod(l)
            hook.start()
            outproj(16, JC, 128, G_ffn, [("g", j) for j in range(JC)], mod(l, 5), t0, TT, xs_v, after_block=lambda: maybe_mod(l), stepper=hook)

        slabs = []
        sub = 0
        for l in range(L):
            if sub >= nsub:
                break
            m = l % 4
            src = xin_v if l == 0 else xs_v
            for q_ in range(4):
                t0 = q_ * 512
                if m == 0:
                    slabs.append((lambda l=l, t0=t0, src=src: mixer_pro(l, t0, src), lambda hook, l=l, q_=q_, t0=t0, src=src: lin_attn(l, "ret", q_, t0, src, hook)))
                elif m == 1:
                    slabs.append((lambda l=l, t0=t0, src=src: mixer_pro(l, t0, src), lambda hook, l=l, q_=q_, t0=t0, src=src: conv_mixer(l, q_, t0, src, hook)))
                elif m == 2:
                    slabs.append((lambda l=l, t0=t0, src=src: mixer_pro(l, t0, src), lambda hook, l=l, q_=q_, t0=t0, src=src: lin_attn(l, "gla", q_, t0, src, hook)))
                else:
                    slabs.append((None, lambda hook, l=l, q_=q_, t0=t0, src=src: pool_mixer(l, q_, t0, src, hook)))
            sub += 1
            if sub >= nsub:
                break
            for hf in range(2):
                slabs.append((lambda l=l, hf=hf: ffn_pro(l, hf), lambda hook, l=l, hf=hf: ffn(l, hf, hook)))
            sub += 1
        class NoStep:
            def start(self):
                pass

            def step(self, n):
                pass

            def finish(self):
                pass

        cur = slabs[0][0]() if (slabs and slabs[0][0] is not None) else None
        if cur is not None:
            cur.finish()
        for i, (pro, body) in enumerate(slabs):
            nxt_f = slabs[i + 1][0] if i + 1 < len(slabs) else None
            st_ = nxt_f() if nxt_f is not None else NoStep()
            body(st_)
            st_.finish()
        assert wstate["next"] == len(plan), (wstate, len(plan))
        xfin = xs_v if slabs else xin_v

        out_toks = []
        osem = [P.sem("o0"), P.sem("o1"), P.sem("o2")]
        for tbg in range(4):
            ta = tbg * 512
            for kc in range(KC):
                xb, xk, xsem = xst.next()
                i = (xst.i - 1) % 3
                P.dma("sync", xb[:], xfin[:, kc, ta:ta + 512], xsem, reads=[("x", kc, tbg)], writes=[xk])
                if final:
                    dve(lambda e, xb=xb, kc=kc, tbg=tbg: e.scalar_tensor_tensor(out=xb[:], in0=xb[:], scalar=fing[:, kc:kc + 1], in1=rv(tbg),
                                                                              op0=ALU.mult, op1=ALU.mult), reads=[xk, "fing", ("rvec", tbg)], writes=[xk])
                out_toks.append(P.dma("sync", out_v[:, kc, ta:ta + 512], xb[:], osem[i], reads=[xk], writes=[("out", kc, ta)]))
        P.wait_all("sync", out_toks)
        P.run()
    return nc


def _lay(v, n):
    return np.ascontiguousarray(np.asarray(v, np.float32).reshape(n, 128).T)


def make_in_maps(inputs, cores):
    f = lambda k: np.asarray(inputs[k])
    shared = {
        "w_mod": np.ascontiguousarray(f("w_mod"), np.float32),
        "b_mod_l": np.ascontiguousarray(np.concatenate([_lay(f("b_mod")[l], 96) for l in range(L)], axis=1)),
        "n1g_l": np.ascontiguousarray(np.concatenate([_lay(f("norm1_g")[l], KC) for l in range(L)], axis=1)),
        "n2g_l": np.ascontiguousarray(np.concatenate([_lay(f("norm2_g")[l], KC) for l in range(L)], axis=1)),
        "fing_l": _lay(f("final_g"), KC),
        "ret_w_in": np.ascontiguousarray(f("ret_w_in")[0], np.float32),
        "ret_w_out": np.ascontiguousarray(f("ret_w_out")[0], np.float32),
        "conv_w_in": np.ascontiguousarray(f("conv_w_in")[0].reshape(D, 3, KC, 128).transpose(0, 2, 1, 3).reshape(D, 3 * D), np.float32),
        "conv_w_l": np.ascontiguousarray(np.concatenate([_lay(f("conv_w")[0][j], KC) for j in range(3)], axis=1)),
        "conv_w_out": np.ascontiguousarray(f("conv_w_out")[0], np.float32),
        "gla_w_in": np.ascontiguousarray(f("gla_w_in")[0], np.float32),
        "gla_w_up": np.ascontiguousarray(f("gla_w_gate_up")[0], np.float32),
        "gla_negb_l": np.ascontiguousarray(_lay(f("gla_b_gate")[0], 8) * np.float32(-1.0)),
        "gla_w_out": np.ascontiguousarray(f("gla_w_out")[0], np.float32),
        "pool_w": np.ascontiguousarray(f("pool_w")[0].reshape(D, 512), np.float32),
        "pool_sc_l": _lay(f("pool_scale")[0], KC),
        "ffn_w_in": np.ascontiguousarray(f("ffn_w_in"), np.float32),
        "ffn_w_out": np.ascontiguousarray(f("ffn_w_out"), np.float32),
        "inv_freq": np.power(np.float32(10000.0), -np.linspace(0.0, 1.0, 128, dtype=np.float32)).astype(np.float32).reshape(128, 1),
        "ident": np.eye(128, dtype=np.float32),
        "m01T": np.triu(np.ones((128, 128), np.float32)),
        "cpos": np.arange(1, 129, dtype=np.float32).reshape(1, 128),
        "invcnt": np.stack([1.0 / np.minimum(np.arange(1, S + 1), w) for w in (2, 4, 8, 16)]).astype(np.float32),
    }
    x, c, pos = f("x"), f("c"), f("positions")
    maps = []
    for b in cores:
        m = dict(shared)
        m["xT"] = np.ascontiguousarray(x[b].T, np.float32)
        m["c_l"] = _lay(c[b], KC)
        m["pos"] = np.ascontiguousarray(pos[b].reshape(1, S), np.int32)
        maps.append(m)
    return maps


_NC_CACHE = {}


def kernel(**inputs):
    n = 8
    if "full" not in _NC_CACHE:
        _NC_CACHE["full"] = build_program()
    nc = _NC_CACHE["full"]
    in_maps = make_in_maps(inputs, list(range(n)))
    res = run_bass_kernel_spmd(nc, in_maps, core_ids=list(range(n)))
    out = np.stack([np.ascontiguousarray(res.results[b]["outT"].T) for b in range(n)], axis=0)
    return out.astype(np.float32)
```

```python
import math
import os
from contextlib import ExitStack

import numpy as np
import concourse.bass as bass
import concourse.mybir as mybir
from concourse.bass_utils import run_bass_kernel_spmd

F32 = mybir.dt.float32
BF16 = mybir.dt.bfloat16
I32 = mybir.dt.int32
AF = mybir.ActivationFunctionType
ALU = mybir.AluOpType
ENGS = ["sync", "scalar", "vector", "gpsimd", "tensor"]
SEM_LIMIT = 16000
SAME_ENGINE_SYNC = {"scalar": True, "vector": True, "gpsimd": True, "tensor": False, "sync": False}

D = 2048
S = 2048
L = 4
KC = 16
DFF = 5632
JC = 44
EPS = 1e-6
PI = math.pi
NSLOT = 3
SLOT = 8192


class Sem:
    __slots__ = ("h", "n", "name", "eng")

    def __init__(self, h, name, eng=None):
        self.h, self.n, self.name, self.eng = h, 0, name, eng


class Prog:
    def __init__(self, nc, stack):
        self.nc, self.stack = nc, stack
        self.q = {e: [] for e in ENGS}
        self.nsem = 0
        self.prog = {e: None for e in ENGS}
        self.last_w = {}
        self.readers = {}
        self.pending = {e: [] for e in ENGS}

    def sem(self, name, eng=None):
        self.nsem += 1
        return Sem(self.stack.enter_context(self.nc.semaphore(f"{name}_{self.nsem}")), name, eng)

    def sbuf(self, name, shape, dt):
        return self.stack.enter_context(self.nc.sbuf_tensor(name, list(shape), dt))

    def psum(self, name, shape, dt=F32):
        return self.stack.enter_context(self.nc.psum_tensor(name, list(shape), dt))

    def _deps(self, eng, reads, writes, extra):
        w = {}

        def add(tok):
            if tok is None:
                return
            s, v = tok
            if s.eng == eng and not SAME_ENGINE_SYNC[eng]:
                return
            if w.get(id(s), (s, 0))[1] < v:
                w[id(s)] = (s, v)

        for k in reads:
            add(self.last_w.get(k))
        for k in writes:
            add(self.last_w.get(k))
            for t in self.readers.get(k, {}).values():
                add(t)
        for t in extra:
            add(t)
        return list(w.values())

    def _register(self, tok, reads, writes):
        s, v = tok
        for k in reads:
            self.readers.setdefault(k, {})[id(s)] = tok
        for k in writes:
            self.last_w[k] = tok
            self.readers[k] = {}

    def op(self, eng, fn, reads=(), writes=(), mark=True, extra=()):
        waits = self._deps(eng, reads, writes, extra)
        tok = None
        sem = None
        if mark:
            sem = self.prog[eng]
            if sem is None or sem.n >= SEM_LIMIT:
                sem = self.prog[eng] = self.sem("pg_" + eng, eng)
            sem.n += 1
            tok = (sem, sem.n)
            for r, wr in self.pending[eng]:
                self._register(tok, r, wr)
            self.pending[eng] = []
            self._register(tok, reads, writes)
        else:
            self.pending[eng].append((tuple(reads), tuple(writes)))
        self.q[eng].append((fn, waits, sem, 1))
        return tok

    def dma(self, eng, out, in_, sem, reads=(), writes=(), extra=()):
        waits = self._deps(eng, reads, writes, extra)
        sem.n += 16
        tok = (sem, sem.n)
        self._register(tok, reads, writes)
        self.q[eng].append((lambda e: e.dma_start(out=out, in_=in_), waits, sem, 16))
        return tok

    def wait_all(self, eng, toks):
        self.q[eng].append((None, self._deps(eng, (), (), toks), None, 0))

    def run(self):
        for e in ENGS:
            assert not self.pending[e], f"unmarked trailing ops on {e}"
        with self.nc.Block() as block:
            for eng in ENGS:
                getattr(block, eng)(lambda e, eng=eng: self._replay(e, eng))

    def _replay(self, e, eng):
        waited = {}
        for fn, waits, sem, by in self.q[eng]:
            for s, v in waits:
                if waited.get(id(s), 0) >= v:
                    continue
                e.wait_ge(s.h, v)
                waited[id(s)] = v
            if fn is None:
                continue
            ins = fn(e)
            if sem is not None:
                ins.then_inc(sem.h, by)


class RR:
    def __init__(self, items):
        self.items, self.i = items, 0

    def next(self):
        it = self.items[self.i % len(self.items)]
        self.i += 1
        return it


def build_program(nsub=2 * L, final=True):
    nc = bass.Bass("TRN2", target_bir_lowering=False)

    def din(name, shape, dt=F32):
        return nc.dram_tensor(name, list(shape), dt, kind="ExternalInput").ap()

    xT_d = din("xT", [D, S])
    c_d = din("c_l", [128, KC])
    pos_d = din("pos", [1, S], I32)
    wmod_d = din("w_mod", [L, D, 6 * D])
    bmod_d = din("b_mod_l", [128, L * 96])
    n1g_d = din("n1g_l", [128, L * KC])
    n2g_d = din("n2g_l", [128, L * KC])
    fing_d = din("fing_l", [128, KC])
    ret_win_d = din("ret_w_in", [D, 12288])
    ret_wout_d = din("ret_w_out", [4096, D])
    conv_win_d = din("conv_w_in", [D, 3 * D])
    convw_d = din("conv_w_l", [128, 3 * KC])
    conv_wout_d = din("conv_w_out", [D, D])
    gla_win_d = din("gla_w_in", [D, 6160])
    gla_wup_d = din("gla_w_up", [16, 1024])
    gla_negb_d = din("gla_negb_l", [128, 8])
    gla_wout_d = din("gla_w_out", [D, D])
    pool_w_d = din("pool_w", [D, 512])
    pool_sc_d = din("pool_sc_l", [128, KC])
    ffn_win_d = din("ffn_w_in", [L, D, 2 * DFF])
    ffn_wout_d = din("ffn_w_out", [L, DFF, D])
    invf_d = din("inv_freq", [128, 1])
    ident_d = din("ident", [128, 128])
    m01_d = din("m01T", [128, 128])
    cpos_d = din("cpos", [1, 128])
    invc_d = din("invcnt", [4, S])
    outT_d = nc.dram_tensor("outT", [D, S], F32, kind="ExternalOutput").ap()
    xs_d = nc.dram_tensor("xs_scr", [D, S], F32).ap()
    cs_d = nc.dram_tensor("cs_scr", [2, 128, S], F32).ap()
    st_d = nc.dram_tensor("st_scr", [8, 128, 1024], F32).ap()

    xin_v = xT_d.rearrange("(kc p) t -> p kc t", p=128)
    xs_v = xs_d.rearrange("(kc p) t -> p kc t", p=128)
    out_v = outT_d.rearrange("(kc p) t -> p kc t", p=128)

    RET_LOGG = [math.log1p(-2.0 ** (-5.0 - h)) for h in range(8)]

    with ExitStack() as st:
        P = Prog(nc, st)
        A = P.sbuf("A", [128, KC, 1024], BF16)
        FLEX = P.sbuf("FLEX", [128, JC * 1024], BF16)
        wslot = [P.sbuf(f"wslot{i}", [128, SLOT], BF16) for i in range(NSLOT)]
        wsem = [P.sem(f"wsem{i}") for i in range(NSLOT)]
        xst = RR([(P.sbuf(f"xst{i}", [128, 512], F32), ("xst", i), P.sem(f"xst{i}")) for i in range(3)])
        xe = RR([(P.sbuf(f"xe{i}", [128, 512], F32), ("xe", i), P.sem(f"xel{i}"), P.sem(f"xes{i}")) for i in range(3)])
        sqb = RR([(P.sbuf(f"sqb{i}", [128, 512], BF16), ("sqb", i)) for i in range(4)])
        stf = RR([(P.sbuf(f"stf{i}", [128, 512], F32), ("stf", i)) for i in range(4)])
        rvec = P.sbuf("rvec", [128, S], F32)
        elast = P.sbuf("elast", [128, 2, 4], F32)
        ones_bf = P.sbuf("ones_bf", [128, 128], BF16)
        ident = P.sbuf("ident_sb", [128, 128], F32)
        m01 = P.sbuf("m01_sb", [128, 128], F32)
        cpos = P.sbuf("cpos_sb", [128, 128], F32)
        invf = P.sbuf("invf_sb", [128, 1], F32)
        c32 = P.sbuf("c32", [128, KC], F32)
        cact = P.sbuf("cact", [128, KC], BF16)
        bmod = P.sbuf("bmod", [128, L * 96], F32)
        modsb = P.sbuf("modsb", [128, L * 96], F32)
        n1g = P.sbuf("n1g", [128, L * KC], F32)
        n2g = P.sbuf("n2g", [128, L * KC], F32)
        fing = P.sbuf("fing", [128, KC], F32)
        avec = P.sbuf("avec", [128, 2 * L * KC], F32)
        poolsc = P.sbuf("poolsc", [128, KC], F32)
        g1p = P.sbuf("g1p", [128, KC], F32)
        convw = P.sbuf("convw", [128, 3 * KC], F32)
        negb = P.sbuf("negb", [128, 8], F32)
        uc_carry = P.sbuf("uc_carry", [128, KC, 2], F32)
        pool_carry = P.sbuf("pool_carry", [128, KC, 16], F32)
        ps = [P.psum(f"ps{i}", [128, 512]) for i in range(8)]
        psr = RR([(ps[i], ("ps", i)) for i in range(8)])
        psr_hi = RR([(ps[i], ("ps", i)) for i in range(4, 8)])
        psr_lo = RR([(ps[i], ("ps", i)) for i in range(0, 4)])
        psr6 = RR([(ps[i], ("ps", i)) for i in range(0, 6)])
        nbanks = [(ps[6], ("ps", 6)), (ps[7], ("ps", 7))]
        msem = P.sem("misc")
        msem_tok = [None]

        def mdma(eng, out, in_, reads=(), writes=()):
            msem_tok[0] = P.dma(eng, out, in_, msem, reads=reads, writes=writes, extra=[msem_tok[0]] if msem_tok[0] else [])
            return msem_tok[0]

        def fx(off, n, dt=BF16):
            v = FLEX[:, off:off + n]
            return v.bitcast(F32) if dt == F32 else v

        G_ffn = FLEX[:, :].rearrange("p (j t) -> p j t", j=JC)
        OG = fx(0, 32 * 512).rearrange("p (j t) -> p j t", j=32)
        off = 32 * 512
        off0 = off
        qdec = fx(off, 1024).rearrange("p (c t) -> p c t", c=2); off += 1024
        kdec = fx(off, 1024).rearrange("p (c t) -> p c t", c=2); off += 1024
        kctok = fx(off, 1024).rearrange("p (c d) -> p c d", c=4); off += 1024
        vtok = fx(off, 2048).rearrange("p (c v) -> p c v", c=4); off += 2048
        gsil = fx(off, 2048).rearrange("p (c t) -> p c t", c=4); off += 2048
        Smask = fx(off, 512).rearrange("p (c t) -> p c t", c=4); off += 512
        sqo = fx(off, 2048).rearrange("p (c t) -> p c t", c=4); off += 2048
        Sbf2 = [fx(off + i * 1024, 1024).rearrange("p (c t) -> p c t", c=2) for i in range(2)]; off += 2048
        bA = fx(off, 2048, F32).rearrange("p (c t) -> p c t", c=2); off += 2048
        bB = fx(off, 2048, F32).rearrange("p (c t) -> p c t", c=2); off += 2048
        Ebuf = fx(off, 2048, F32).rearrange("p (c t) -> p c t", c=2); off += 2048
        qrot = fx(off, 2048, F32).rearrange("p (c t) -> p c t", c=2); off += 2048
        krot = fx(off, 2048, F32).rearrange("p (c t) -> p c t", c=2); off += 2048
        S32 = fx(off, 2048, F32).rearrange("p (c t) -> p c t", c=2); off += 2048
        cs_sb = fx(off, 2048, F32).rearrange("p (c t) -> p c t", c=2)
        zT = fx(off, 1024, F32)[0:16, :]; off += 1024
        wup = fx(off, 2048, F32)[0:16, :]; off += 2048
        rhead = fx(off, 1024, F32); off += 1024
        assert off <= JC * 1024, off
        off = off0
        ucx = fx(off, 1056, F32)[:, 0:514]; off += 1056
        hx = RR([(fx(off + i * 1056, 1056, F32), ("hx", i)) for i in range(2)]); off += 2 * 1056
        Wp = [fx(off + i * 1056, 1056, F32) for i in range(4)]; off += 4 * 1056
        invc = fx(off, 4096, F32).rearrange("p (c t) -> p c t", c=4); off += 4096
        assert off <= JC * 1024

        plan = []
        wstate = {"issued": 0, "next": 0}

        def w_issue():
            i = wstate["issued"]
            if i >= len(plan):
                return
            s = i % NSLOT
            for src, dstf in plan[i]:
                P.dma("gpsimd", dstf(wslot[s]), src, wsem[s], writes=[("w", s)])
            wstate["issued"] += 1

        def w_get():
            i = wstate["next"]
            wstate["next"] += 1
            while wstate["issued"] < min(i + NSLOT, len(plan)) and wstate["issued"] <= i:
                w_issue()
            return wslot[i % NSLOT], ("w", i % NSLOT)

        def w_done():
            while wstate["issued"] < min(wstate["next"] - 1 + NSLOT + 1, len(plan)):
                w_issue()

        def view(slot, kcn, fb):
            return slot[:, 0:kcn * fb].rearrange("p (k f) -> p k f", f=fb)

        def blk_cols(w2d, kcn, colranges):
            wv = w2d.rearrange("(kc p) f -> p kc f", p=128)
            fb = sum(b - a for a, b in colranges)
            descs = []
            o = 0
            for a, b in colranges:
                descs.append((wv[:, :, a:b], (lambda slot, o=o, n=b - a: view(slot, kcn, fb)[:, :, o:o + n])))
                o += b - a
            return descs

        def plan_all():
            for b in range(8):
                plan.append(blk_cols(wmod_d[0], KC, [(b * 512, (b + 1) * 512)]))
            pm = {"cnt": 0, "b": 0}

            def plan_maybe_mod(l):
                if l + 1 >= L or pm["b"] >= 24:
                    return
                pm["cnt"] += 1
                if pm["cnt"] % 2 == 0:
                    plan.append(blk_cols(wmod_d[l + 1], KC, [(pm["b"] * 512, (pm["b"] + 1) * 512)]))
                    pm["b"] += 1
            sub = 0
            for l in range(L):
                if sub >= nsub:
                    break
                m = l % 4
                for q in range(4):
                    if m == 0:
                        for hd in range(8):
                            plan.append(blk_cols(ret_win_d, KC, [(hd * 256, hd * 256 + 256), (2048 + hd * 256, 2048 + hd * 256 + 256)]))
                            plan.append(blk_cols(ret_win_d, KC, [(4096 + hd * 512, 4096 + hd * 512 + 512)]))
                            plan.append(blk_cols(ret_win_d, KC, [(8192 + hd * 512, 8192 + hd * 512 + 512)]))
                            if l == 0 and q == 0:
                                for b in (8 + 2 * hd, 9 + 2 * hd):
                                    plan.append(blk_cols(wmod_d[0], KC, [(b * 512, (b + 1) * 512)]))
                        for b in range(8):
                            plan.append(blk_cols(ret_wout_d, 32, [(b * 256, b * 256 + 256)]))
                    elif m == 1:
                        for fc in range(16):
                            plan.append(blk_cols(conv_win_d, KC, [(fc * 384, fc * 384 + 384)]))
                        for b in range(4):
                            plan.append(blk_cols(conv_wout_d, KC, [(b * 512, b * 512 + 512)]))
                    elif m == 2:
                        plan.append(blk_cols(gla_win_d, KC, [(6144, 6160)]))
                        for hd in range(4):
                            plan.append(blk_cols(gla_win_d, KC, [(hd * 256, hd * 256 + 256), (1024 + hd * 256, 1024 + hd * 256 + 256)]))
                            plan.append(blk_cols(gla_win_d, KC, [(2048 + hd * 512, 2048 + hd * 512 + 512)]))
                            plan.append(blk_cols(gla_win_d, KC, [(4096 + hd * 512, 4096 + hd * 512 + 512)]))
                        for b in range(4):
                            plan.append(blk_cols(gla_wout_d, KC, [(b * 512, b * 512 + 512)]))
                    else:
                        for g in range(4):
                            plan.append(blk_cols(pool_w_d[g * 512:(g + 1) * 512, :], 4, [(0, 512)]))
                sub += 1
                if sub >= nsub:
                    break
                pm["cnt"], pm["b"] = 0, 0
                for hf in range(2):
                    for b in range(22):
                        plan.append(blk_cols(ffn_win_d[l], KC, [(b * 256, b * 256 + 256), (DFF + b * 256, DFF + b * 256 + 256)]))
                        plan_maybe_mod(l)
                    for i in range(16):
                        plan.append(blk_cols(ffn_wout_d[l], JC, [(i * 128, i * 128 + 128)]))
                        plan_maybe_mod(l)
                sub += 1

        plan_all()

        def act(fn, reads=(), writes=()):
            return P.op("scalar", fn, reads=reads, writes=writes)

        def dve(fn, reads=(), writes=()):
            return P.op("vector", fn, reads=reads, writes=writes)

        def pe(fn, reads=(), writes=(), mark=False):
            return P.op("tensor", fn, reads=reads, writes=writes, mark=mark)

        def rsqrt_from_psum(pb, pk, dst, dkey, scale):
            act(lambda e: e.activation(out=dst, in_=pb[:], func=AF.Ln, bias=EPS, scale=scale), reads=[pk], writes=[dkey])
            act(lambda e: e.activation(out=dst, in_=dst, func=AF.Exp, scale=-0.5), reads=[dkey], writes=[dkey])

        def rv(tbg):
            return rvec[:, tbg * 512:(tbg + 1) * 512]

        def norm_sums(t0, tt, src):
            for tb in range(tt // 512):
                pb, pk = psr.next()
                ta = t0 + tb * 512
                for kc in range(KC):
                    xb, xk, xsem = xst.next()
                    P.dma("sync", xb[:], src[:, kc, ta:ta + 512], xsem, reads=[("x", kc, ta // 512)], writes=[xk])
                    sb, sk = sqb.next()
                    act(lambda e, sb=sb, xb=xb: e.activation(out=sb[:], in_=xb[:], func=AF.Square), reads=[xk], writes=[sk])
                    pe(lambda e, pb=pb, sb=sb, kc=kc: e.matmul(pb[:], ones_bf[:], sb[:], start=(kc == 0), stop=(kc == KC - 1)),
                       reads=[sk, "ones"], writes=[pk], mark=True)
                rsqrt_from_psum(pb, pk, rv(ta // 512), ("rvec", ta // 512), 1.0 / D)

        def prologue(t0, tt, a_ap, s_ap, src):
            for tb in range(tt // 512):
                ta = t0 + tb * 512
                tbg = ta // 512
                sl = slice(tb * 512, (tb + 1) * 512)
                for kc in range(KC):
                    xb, xk, xsem = xst.next()
                    P.dma("sync", xb[:], src[:, kc, ta:ta + 512], xsem, reads=[("x", kc, tbg)], writes=[xk])
                    dve(lambda e, xb=xb, kc=kc, tbg=tbg: e.scalar_tensor_tensor(out=xb[:], in0=xb[:], scalar=a_ap[:, kc:kc + 1], in1=rv(tbg),
                                                                                op0=ALU.mult, op1=ALU.mult), reads=[xk, "vecs", ("rvec", tbg)], writes=[xk])
                    act(lambda e, xb=xb, kc=kc, sl=sl: e.activation(out=A[:, kc, sl], in_=xb[:], func=AF.Identity, bias=s_ap[:, kc:kc + 1], scale=1.0),
                        reads=[xk, "vecs"], writes=[("h", kc, tb)])
                    yield

        class Stepper:
            def __init__(self, gen, total):
                self.gen, self.total, self.started = gen, total, False

            def start(self):
                self.started = True

            def step(self, ngroups):
                if not self.started or self.gen is None:
                    return
                for _ in range(-(-self.total // ngroups)):
                    if next(self.gen, "end") == "end":
                        self.gen = None
                        return

            def finish(self):
                if self.gen is not None:
                    for _ in self.gen:
                        pass
                    self.gen = None

        def gemm_fm(pb, pk, wv, wk, fsl, actv, akeys, kcn, tsl, mark=True):
            for kc in range(kcn):
                pe(lambda e, kc=kc: e.matmul(pb, wv[:, kc, fsl], actv[:, kc, tsl], start=(kc == 0), stop=(kc == kcn - 1)),
                   reads=[wk, akeys[kc]], writes=[pk], mark=(mark and kc == kcn - 1))

        def x_load(fc, tbg, src):
            eb, ek, lsem, ssem = xe.next()
            P.dma("sync", eb[:], src[:, fc, tbg * 512:(tbg + 1) * 512], lsem, reads=[("x", fc, tbg)], writes=[ek])
            return eb, ek, ssem

        def x_epilogue(pb, pk, gate_col, fc, tbg, loaded, nb):
            eb, ek, ssem = loaded
            dve(lambda e: e.scalar_tensor_tensor(out=eb[:], in0=pb[:], scalar=gate_col, in1=eb[:], op0=ALU.mult, op1=ALU.add),
                reads=[pk, ek, "vecs"], writes=[ek])
            P.dma("sync", xs_v[:, fc, tbg * 512:(tbg + 1) * 512], eb[:], ssem, reads=[ek], writes=[("x", fc, tbg)])
            sb, sk = sqb.next()
            act(lambda e: e.activation(out=sb[:], in_=eb[:], func=AF.Square), reads=[ek], writes=[sk])
            nbp, nbk = nb

            def deferred():
                pe(lambda e: e.matmul(nbp[:], ones_bf[:], sb[:], start=(fc == 0), stop=(fc == KC - 1)), reads=[sk, "ones"], writes=[nbk], mark=True)
                if fc == KC - 1:
                    rsqrt_from_psum(nbp, nbk, rv(tbg), ("rvec", tbg), 1.0 / D)
            return deferred

        def outproj(nblk, kcn, fb, actv, akeys, gate_ap, t0, tt, src, fc_of=None, after_block=None, stepper=None):
            ntb = tt // 512
            nfc = fb // 128
            ngroups = nblk * nfc * ntb
            glist = [(b * nfc + f if fc_of is None else fc_of(b, f), t0 // 512 + tb) for b in range(nblk) for f in range(nfc) for tb in range(ntb)]
            loads = []

            def ensure_loaded(upto):
                while len(loads) <= min(upto, ngroups - 1):
                    fc_, tbg_ = glist[len(loads)]
                    loads.append(x_load(fc_, tbg_, src))

            defq = []
            gi = 0
            for b in range(nblk):
                slot, wk = w_get()
                wv = view(slot, kcn, fb)
                for f in range(nfc):
                    fc = b * nfc + f if fc_of is None else fc_of(b, f)
                    for tb in range(ntb):
                        ensure_loaded(gi + 1)
                        pb, pk = psr6.next()
                        a_v, a_k = (actv, akeys) if fc_of is None else actv(b)
                        gemm_fm(pb[:], pk, wv, wk, slice(f * 128, (f + 1) * 128), a_v, a_k, kcn, slice(tb * 512, (tb + 1) * 512))
                        defq.append(x_epilogue(pb, pk, gate_ap[:, fc:fc + 1], fc, t0 // 512 + tb, loads[gi], nbanks[tb]))
                        gi += 1
                        if len(defq) > 2:
                            defq.pop(0)()
                        if stepper is not None:
                            stepper.step(ngroups)
                w_done()
                if after_block is not None:
                    after_block()
            for d_ in defq:
                d_()

        for dst, src, key in [(ident, ident_d, "ident"), (m01, m01_d, "m01"), (invf, invf_d, "invf"), (c32, c_d, "c32"), (bmod, bmod_d, "bmod"),
                              (n1g, n1g_d, "n1g"), (n2g, n2g_d, "n2g"), (fing, fing_d, "fing"), (poolsc, pool_sc_d, "poolsc"),
                              (convw, convw_d, "vecs0"), (negb, gla_negb_d, "negb")]:
            mdma("sync", dst[:], src, writes=[key])
        mdma("sync", cpos[:], cpos_d.partition_broadcast(128), writes=["cpos"])
        dve(lambda e: e.memset(ones_bf[:], 1.0), writes=["ones"])
        dve(lambda e: e.memset(uc_carry[:], 0.0), writes=["uc_carry"])
        dve(lambda e: e.memset(pool_carry[:], 0.0), writes=["pool_carry"])
        act(lambda e: e.activation(out=cact[:], in_=c32[:], func=AF.Silu), reads=["c32"], writes=["cact"])
        HI = 6.28125
        LO = 2 * PI - HI
        posi = xst.items[0][0].bitcast(I32)
        angb = xst.items[1][0]
        an = xst.items[2][0]
        nI = stf.items[0][0].bitcast(I32)
        nF = stf.items[1][0]
        rr_ = stf.items[2][0]
        kposi, kang, kan, knI, knF, krr = ("xst", 0), ("xst", 1), ("xst", 2), ("stf", 0), ("stf", 1), ("stf", 2)
        for blk in range(S // 512):
            tsl = slice(blk * 512, (blk + 1) * 512)
            mdma("sync", posi[:], pos_d[:, tsl].partition_broadcast(128), writes=[kposi])
            dve(lambda e: e.tensor_copy(out=angb[:], in_=posi[:]), reads=[kposi], writes=[kang])
            dve(lambda e: e.tensor_scalar(out=angb[:], in0=angb[:], scalar1=invf[:, 0:1], scalar2=None, op0=ALU.mult), reads=[kang, "invf"], writes=[kang])
            for which, shift in ((0, 0.5 * PI), (1, 0.0)):
                dve(lambda e, shift=shift: e.tensor_scalar(out=an[:], in0=angb[:], scalar1=shift, scalar2=None, op0=ALU.add), reads=[kang], writes=[kan])
                dve(lambda e: e.tensor_scalar(out=nF[:], in0=an[:], scalar1=1.0 / (2 * PI), scalar2=None, op0=ALU.mult), reads=[kan], writes=[knF])
                dve(lambda e: e.tensor_copy(out=nI[:], in_=nF[:]), reads=[knF], writes=[knI])
                dve(lambda e: e.tensor_copy(out=nF[:], in_=nI[:]), reads=[knI], writes=[knF])
                dve(lambda e: e.scalar_tensor_tensor(out=rr_[:], in0=nF[:], scalar=-HI, in1=an[:], op0=ALU.mult, op1=ALU.add), reads=[knF, kan], writes=[krr])
                dve(lambda e: e.scalar_tensor_tensor(out=rr_[:], in0=nF[:], scalar=-LO, in1=rr_[:], op0=ALU.mult, op1=ALU.add), reads=[knF, krr], writes=[krr])
                dve(lambda e: e.tensor_scalar(out=nF[:], in0=rr_[:], scalar1=PI, scalar2=2 * PI, op0=ALU.is_gt, op1=ALU.mult), reads=[krr], writes=[knF])
                dve(lambda e: e.tensor_tensor(out=rr_[:], in0=rr_[:], in1=nF[:], op=ALU.subtract), reads=[knF, krr], writes=[krr])
                dve(lambda e: e.tensor_scalar(out=nF[:], in0=rr_[:], scalar1=-PI, scalar2=2 * PI, op0=ALU.is_lt, op1=ALU.mult), reads=[krr], writes=[knF])
                dve(lambda e: e.tensor_tensor(out=rr_[:], in0=rr_[:], in1=nF[:], op=ALU.add), reads=[knF, krr], writes=[krr])
                act(lambda e: e.activation(out=rr_[:], in_=rr_[:], func=AF.Sin), reads=[krr], writes=[krr])
                mdma("sync", cs_d[which, :, tsl], rr_[:], reads=[krr], writes=[("cs", which)])
        def mod(l, j):
            return modsb[:, l * 96 + j * 16: l * 96 + (j + 1) * 16]

        def mod_block(l, b):
            slot, wk = w_get()
            wv = view(slot, KC, 512)
            pb, pk = psr6.next()
            for f in range(4):
                for kc in range(KC):
                    pe(lambda e, f=f, kc=kc: e.matmul(pb[:, f:f + 1], wv[:, kc, f * 128:(f + 1) * 128], cact[:, kc:kc + 1], start=(kc == 0), stop=(kc == KC - 1)),
                       reads=[wk, "cact"], writes=[pk], mark=(f == 3 and kc == KC - 1))
            w_done()
            c0 = l * 96 + b * 4
            dve(lambda e: e.tensor_tensor(out=modsb[:, c0:c0 + 4], in0=pb[:, 0:4], in1=bmod[:, c0:c0 + 4], op=ALU.add), reads=[pk, "bmod"], writes=["modsb"])

        def mod_finish(l, parts=(0, 1)):
            for which, (gsrc, jsc) in enumerate(((n1g, 1), (n2g, 4))):
                if which not in parts:
                    continue
                dst = avec[:, (l * 2 + which) * KC:(l * 2 + which + 1) * KC]
                dve(lambda e, dst=dst, jsc=jsc: e.tensor_scalar(out=dst, in0=mod(l, jsc), scalar1=1.0, scalar2=None, op0=ALU.add), reads=["modsb"], writes=["vecs"])
                dve(lambda e, dst=dst, gsrc=gsrc: e.tensor_tensor(out=dst, in0=dst, in1=gsrc[:, l * KC:(l + 1) * KC], op=ALU.mult), reads=["vecs", "n1g", "n2g"], writes=["vecs"])
            if l == 3:
                dve(lambda e: e.tensor_tensor(out=g1p[:], in0=mod(3, 2), in1=poolsc[:], op=ALU.mult), reads=["modsb", "poolsc"], writes=["vecs"])

        norm_sums(0, S, xin_v)
        for b in range(8):
            mod_block(0, b)
        mod_finish(0, parts=(0,))
        dve(lambda e: e.tensor_copy(out=convw[:], in_=convw[:]), reads=["vecs0"], writes=["vecs"])
        modst = {"cnt": 0, "b": 0}

        def maybe_mod(l):
            if l + 1 >= L or modst["b"] >= 24:
                return
            modst["cnt"] += 1
            if modst["cnt"] % 2 == 0:
                mod_block(l + 1, modst["b"])
                modst["b"] += 1
                if modst["b"] == 24:
                    mod_finish(l + 1)

        def mixer_pro(l, t0, src):
            return Stepper(prologue(t0, 512, avec[:, (l * 2) * KC:(l * 2 + 1) * KC], mod(l, 0), src), 16)

        def ffn_pro(l, hf):
            return Stepper(prologue(hf * 1024, 1024, avec[:, (l * 2 + 1) * KC:(l * 2 + 2) * KC], mod(l, 3), xs_v), 32)

        def lin_attn(l, kind, q_, t0, src, hook):
            TT = 512
            nh = 8 if kind == "ret" else 4
            hk = [("h", kc, 0) for kc in range(KC)]
            bA4 = bA.rearrange("p c (n t) -> p c n t", n=4)
            bB4 = bB.rearrange("p c (n t) -> p c n t", n=4)
            E4 = Ebuf.rearrange("p c (n t) -> p c n t", n=4)
            krot4 = krot.rearrange("p c (n t) -> p c n t", n=4)
            if kind == "ret":
                mdma("sync", cs_sb[:, 0, :], cs_d[0, :, t0:t0 + TT], reads=[("cs", 0)], writes=["cs_sb"])
                mdma("sync", cs_sb[:, 1, :], cs_d[1, :, t0:t0 + TT], reads=[("cs", 1)], writes=["cs_sb"])
            else:
                mdma("sync", wup, gla_wup_d, writes=["wup"])
                slot, wk = w_get()
                wv = view(slot, KC, 16)
                pb, pk = psr.next()
                gemm_fm(pb[0:16, :], pk, wv, wk, slice(0, 16), A, hk, KC, slice(0, TT))
                w_done()
                act(lambda e, pb=pb: e.activation(out=zT, in_=pb[0:16, :], func=AF.Copy), reads=[pk], writes=["zT"])
            for hd in range(nh):
                if kind == "ret":
                    dve(lambda e, hd=hd: e.tensor_scalar(out=bA4, in0=cpos[:, :].unsqueeze(1).unsqueeze(1).to_broadcast([128, 2, 4, 128]), scalar1=RET_LOGG[hd],
                                                         scalar2=None, op0=ALU.mult), reads=["cpos"], writes=["bA"])
                    bfin, bkey = bA, "bA"
                else:
                    for dc in range(2):
                        pb, pk = psr.next()
                        pe(lambda e, pb=pb, hd=hd, dc=dc: e.matmul(pb[:], wup[:, (hd * 2 + dc) * 128:(hd * 2 + dc + 1) * 128], zT, start=True, stop=True),
                           reads=["wup", "zT"], writes=[pk], mark=True)
                        act(lambda e, pb=pb, hd=hd, dc=dc: e.activation(out=bA[:, dc, :], in_=pb[:], func=AF.Exp, bias=negb[:, hd * 2 + dc:hd * 2 + dc + 1], scale=-1.0),
                            reads=[pk, "negb"], writes=["bA"])
                    act(lambda e: e.activation(out=bA, in_=bA, func=AF.Ln, bias=1.0, scale=1.0), reads=["bA"], writes=["bA"])
                    dve(lambda e: e.tensor_scalar(out=bA, in0=bA, scalar1=-1.0 / 16.0, scalar2=None, op0=ALU.mult), reads=["bA"], writes=["bA"])
                    src4, dst4, sk_, dk_ = bA4, bB4, "bA", "bB"
                    for sh in (1, 2, 4, 8, 16, 32, 64):
                        dve(lambda e, src4=src4, dst4=dst4, sh=sh: e.tensor_tensor(out=dst4[:, :, :, sh:], in0=src4[:, :, :, sh:], in1=src4[:, :, :, :128 - sh], op=ALU.add),
                            reads=[sk_], writes=[dk_])
                        act(lambda e, src4=src4, dst4=dst4, sh=sh: e.activation(out=dst4[:, :, :, :sh], in_=src4[:, :, :, :sh], func=AF.Copy), reads=[sk_], writes=[dk_])
                        src4, dst4, sk_, dk_ = dst4, src4, dk_, sk_
                    bfin, bkey = bB, "bB"
                slot, wk = w_get()
                wv = view(slot, KC, 512)
                banks = [psr_hi.next() for _ in range(4)]
                for f, (pb, pk) in enumerate(banks):
                    gemm_fm(pb[:], pk, wv, wk, slice(f * 128, (f + 1) * 128), A, hk, KC, slice(0, TT))
                w_done()
                if kind == "ret":
                    for (b0, b1), dst, dkey, sc in (((banks[0], banks[1]), qrot, "qrot", 1.0), ((banks[2], banks[3]), krot, "krot", 1.0 / 16.0)):
                        t1, k1 = stf.next()
                        t2, k2 = stf.next()
                        m1, km1 = stf.next()
                        m2, km2 = stf.next()
                        act(lambda e, t1=t1, b0=b0, sc=sc: e.activation(out=t1[:], in_=b0[0][:], func=AF.Copy, scale=sc), reads=[b0[1]], writes=[k1])
                        act(lambda e, t2=t2, b1=b1, sc=sc: e.activation(out=t2[:], in_=b1[0][:], func=AF.Copy, scale=sc), reads=[b1[1]], writes=[k2])
                        dve(lambda e, t1=t1, m1=m1: e.tensor_tensor(out=m1[:], in0=t1[:], in1=cs_sb[:, 0, :], op=ALU.mult), reads=[k1, "cs_sb"], writes=[km1])
                        dve(lambda e, t2=t2, m2=m2: e.tensor_tensor(out=m2[:], in0=t2[:], in1=cs_sb[:, 1, :], op=ALU.mult), reads=[k2, "cs_sb"], writes=[km2])
                        dve(lambda e, m1=m1, m2=m2, dst=dst: e.tensor_tensor(out=dst[:, 0, :], in0=m1[:], in1=m2[:], op=ALU.subtract), reads=[km1, km2], writes=[dkey])
                        dve(lambda e, t2=t2, m1=m1: e.tensor_tensor(out=m1[:], in0=t2[:], in1=cs_sb[:, 0, :], op=ALU.mult), reads=[k2, "cs_sb"], writes=[km1])
                        dve(lambda e, t1=t1, m2=m2: e.tensor_tensor(out=m2[:], in0=t1[:], in1=cs_sb[:, 1, :], op=ALU.mult), reads=[k1, "cs_sb"], writes=[km2])
                        dve(lambda e, m1=m1, m2=m2, dst=dst: e.tensor_tensor(out=dst[:, 1, :], in0=m1[:], in1=m2[:], op=ALU.add), reads=[km1, km2], writes=[dkey])
                else:
                    for f, (dst, dkey, sc) in enumerate(((qrot, "qrot", 1.0 / 16.0), (qrot, "qrot", 1.0 / 16.0), (krot, "krot", 1.0), (krot, "krot", 1.0))):
                        pb, pk = banks[f]
                        act(lambda e, pb=pb, dst=dst, f=f, sc=sc: e.activation(out=dst[:, f % 2, :], in_=pb[:], func=AF.Copy, scale=sc), reads=[pk], writes=[dkey])
                act(lambda e, bfin=bfin: e.activation(out=Ebuf, in_=bfin, func=AF.Exp), reads=[bkey], writes=["E"])
                dve(lambda e: e.tensor_tensor(out=qdec, in0=qrot, in1=Ebuf, op=ALU.mult), reads=["qrot", "E"], writes=["qdec"])
                dve(lambda e: e.tensor_copy(out=elast[:, :, :].unsqueeze(3), in_=E4[:, :, :, 127:128]), reads=["E"], writes=["elast"])
                act(lambda e, bfin=bfin: e.activation(out=Ebuf, in_=bfin, func=AF.Exp, scale=-1.0), reads=[bkey], writes=["E"])
                dve(lambda e: e.tensor_tensor(out=krot, in0=krot, in1=Ebuf, op=ALU.mult), reads=["krot", "E"], writes=["krot"])
                act(lambda e: e.activation(out=kdec, in_=krot, func=AF.Copy), reads=["krot"], writes=["kdec"])
                dve(lambda e: e.tensor_tensor(out=krot4, in0=krot4, in1=elast[:, :, :].unsqueeze(3).to_broadcast([128, 2, 4, 128]), op=ALU.mult),
                    reads=["krot", "elast"], writes=["krot"])
                slot, wk = w_get()
                wv = view(slot, KC, 512)
                for tt_ in range(4):
                    pb, pk = psr_lo.next()
                    for kc in range(KC):
                        pe(lambda e, pb=pb, wv=wv, kc=kc, tt_=tt_: e.matmul(pb[:], A[:, kc, tt_ * 128:(tt_ + 1) * 128], wv[:, kc, :], start=(kc == 0), stop=(kc == KC - 1)),
                           reads=[wk, ("h", kc, 0)], writes=[pk], mark=(kc == KC - 1))
                    if tt_ % 2 == 0:
                        act(lambda e, pb=pb, tt_=tt_: e.activation(out=vtok[:, tt_, :], in_=pb[:], func=AF.Copy), reads=[pk], writes=["vtok"])
                    else:
                        dve(lambda e, pb=pb, tt_=tt_: e.tensor_copy(out=vtok[:, tt_, :], in_=pb[:]), reads=[pk], writes=["vtok"])
                w_done()
                slot, wk = w_get()
                wv = view(slot, KC, 512)
                for f in range(4):
                    pb, pk = psr_lo.next()
                    gemm_fm(pb[:], pk, wv, wk, slice(f * 128, (f + 1) * 128), A, hk, KC, slice(0, TT))
                    act(lambda e, pb=pb, f=f: e.activation(out=gsil[:, f, :], in_=pb[:], func=AF.Silu), reads=[pk], writes=["gsil"])
                w_done()
                if l == 0 and q_ == 0:
                    mod_block(0, 8 + 2 * hd)
                    mod_block(0, 9 + 2 * hd)
                    if hd == nh - 1:
                        mod_finish(0, parts=(1,))
                if hd == nh - 1:
                    hook.start()
                for c2 in range(2):
                    pb, pk = psr_hi.next()
                    for cc in range(2):
                        c = c2 * 2 + cc
                        for dc in range(2):
                            pe(lambda e, pb=pb, cc=cc, dc=dc, c=c: e.transpose(pb[:, cc * 256 + dc * 128: cc * 256 + (dc + 1) * 128], krot[:, dc, c * 128:(c + 1) * 128], ident[:]),
                               reads=["krot", "ident"], writes=[pk], mark=(cc == 1 and dc == 1))
                    act(lambda e, pb=pb, c2=c2: e.activation(out=kctok[:, c2 * 2:c2 * 2 + 2, :], in_=pb[:].rearrange("p (c d) -> p c d", c=2), func=AF.Copy),
                        reads=[pk], writes=["kctok"])
                if q_ == 0:
                    dve(lambda e: e.memset(S32, 0.0), writes=["S32"])
                else:
                    mdma("sync", S32, st_d[hd].rearrange("p (c v) -> p c v", c=2), reads=[("st", hd)], writes=["S32"])
                act(lambda e: e.activation(out=Sbf2[0], in_=S32, func=AF.Copy), reads=["S32"], writes=[("Sbf", 0)])
                pb, pk = psr_hi.next()
                for c in range(4):
                    csl = slice(c * 128, (c + 1) * 128)
                    for dc in range(2):
                        pe(lambda e, pb=pb, csl=csl, dc=dc: e.matmul(pb[:, csl], kdec[:, dc, csl], qdec[:, dc, csl], start=(dc == 0), stop=(dc == 1)),
                           reads=["kdec", "qdec"], writes=[pk], mark=(c == 3 and dc == 1))
                dve(lambda e, pb=pb: e.tensor_tensor(out=Smask, in0=pb[:].rearrange("p (c t) -> p c t", c=4), in1=m01[:, :].unsqueeze(1).to_broadcast([128, 4, 128]), op=ALU.mult),
                    reads=[pk, "m01"], writes=["Smask"])
                okeys = [("ps", vc) for vc in range(4)]
                for c in range(4):
                    csl = slice(c * 128, (c + 1) * 128)
                    Scur, kcur = Sbf2[c % 2], ("Sbf", c % 2)
                    Snxt, knxt = Sbf2[(c + 1) % 2], ("Sbf", (c + 1) % 2)
                    kvb = []
                    for dc in range(2):
                        pb, pk = psr_hi.next()
                        kvb.append((pb, pk))
                        pe(lambda e, pb=pb, c=c, dc=dc: e.matmul(pb[:], kctok[:, c, dc * 128:(dc + 1) * 128], vtok[:, c, :], start=True, stop=True),
                           reads=["kctok", "vtok"], writes=[pk], mark=True)
                    for vc in range(4):
                        vsl = slice(vc * 128, (vc + 1) * 128)
                        pe(lambda e, vc=vc, c=c, csl=csl, vsl=vsl: e.matmul(ps[vc][:, csl], vtok[:, c, vsl], Smask[:, c, :], start=True, stop=False),
                           reads=["vtok", "Smask"], writes=[okeys[vc]])
                        pe(lambda e, vc=vc, csl=csl, vsl=vsl, Scur=Scur: e.matmul(ps[vc][:, csl], Scur[:, 0, vsl], qdec[:, 0, csl], start=False, stop=False),
                           reads=[kcur, "qdec"], writes=[okeys[vc]])
                        pe(lambda e, vc=vc, csl=csl, vsl=vsl, Scur=Scur: e.matmul(ps[vc][:, csl], Scur[:, 1, vsl], qdec[:, 1, csl], start=False, stop=True),
                           reads=[kcur, "qdec"], writes=[okeys[vc]], mark=(vc == 3))
                    if c < 3:
                        for dc in range(2):
                            pb, pk = kvb[dc]
                            dve(lambda e, pb=pb, c=c, dc=dc, Snxt=Snxt: e.scalar_tensor_tensor(out=Snxt[:, dc, :], in0=S32[:, dc, :], scalar=elast[:, dc, c:c + 1], in1=pb[:],
                                                                                         op0=ALU.mult, op1=ALU.add), reads=[pk, "S32", "elast"], writes=[knxt])
                    for dc in range(2):
                        pb, pk = kvb[dc]
                        dve(lambda e, pb=pb, c=c, dc=dc: e.scalar_tensor_tensor(out=S32[:, dc, :], in0=S32[:, dc, :], scalar=elast[:, dc, c:c + 1], in1=pb[:],
                                                                                op0=ALU.mult, op1=ALU.add), reads=[pk, "S32", "elast"], writes=["S32"])
                mdma("sync", st_d[hd].rearrange("p (c v) -> p c v", c=2), S32, reads=["S32"], writes=[("st", hd)])
                for vc in range(4):
                    act(lambda e, vc=vc: e.activation(out=sqo[:, vc, :], in_=ps[vc][:], func=AF.Square), reads=[okeys[vc]], writes=["sqo"])
                pb, pk = psr_hi.next()
                for vc in range(4):
                    pe(lambda e, pb=pb, vc=vc: e.matmul(pb[:], ones_bf[:], sqo[:, vc, :], start=(vc == 0), stop=(vc == 3)), reads=["sqo", "ones"], writes=[pk], mark=(vc == 3))
                rsqrt_from_psum(pb, pk, rhead, "rhead", 1.0 / 512.0)
                for vc in range(4):
                    tb_, ktb = stf.next()
                    dve(lambda e, vc=vc, tb_=tb_: e.tensor_tensor(out=tb_[:], in0=ps[vc][:], in1=rhead, op=ALU.mult), reads=[okeys[vc], "rhead"], writes=[ktb])
                    dve(lambda e, vc=vc, hd=hd, tb_=tb_: e.tensor_tensor(out=OG[:, hd * 4 + vc, :], in0=tb_[:], in1=gsil[:, vc, :], op=ALU.mult),
                        reads=[ktb, "gsil"], writes=[("og", hd * 4 + vc)])
            kcn = nh * 4
            outproj(8 if kind == "ret" else 4, kcn, SLOT // kcn, OG, [("og", j) for j in range(kcn)], mod(l, 2), t0, TT, src, stepper=hook)

        def conv_mixer(l, q_, t0, src, hook):
            TT = 512
            hk = [("h", kc, 0) for kc in range(KC)]
            YB = OG
            for fc in range(KC):
                slot, wk = w_get()
                wv = view(slot, KC, 384)
                banks = [psr.next() for _ in range(3)]
                for g, (pb, pk) in enumerate(banks):
                    gemm_fm(pb[:], pk, wv, wk, slice(g * 128, (g + 1) * 128), A, hk, KC, slice(0, TT))
                w_done()
                (pbb, pkb), (pbc, pkc), (pbu, pku) = banks
                cs_, kcs = stf.next()
                y_, ky = stf.next()
                act(lambda e, cs_=cs_, pbc=pbc: e.activation(out=cs_[:], in_=pbc[:], func=AF.Copy), reads=[pkc], writes=[kcs])
                act(lambda e, fc=fc: e.activation(out=ucx[:, 0:2], in_=uc_carry[:, fc, :], func=AF.Copy), reads=["uc_carry"], writes=["ucx"])
                dve(lambda e, cs_=cs_, pbu=pbu: e.tensor_tensor(out=ucx[:, 2:514], in0=cs_[:], in1=pbu[:], op=ALU.mult), reads=[kcs, pku, "ucx"], writes=["ucx"])
                act(lambda e, y_=y_, fc=fc: e.activation(out=y_[:], in_=ucx[:, 2:514], func=AF.Copy, scale=convw[:, 2 * KC + fc:2 * KC + fc + 1]), reads=["ucx", "vecs"], writes=[ky])
                dve(lambda e, y_=y_, fc=fc: e.scalar_tensor_tensor(out=y_[:], in0=ucx[:, 1:513], scalar=convw[:, KC + fc:KC + fc + 1], in1=y_[:], op0=ALU.mult, op1=ALU.add),
                    reads=["ucx", ky, "vecs"], writes=[ky])
                dve(lambda e, y_=y_, fc=fc: e.scalar_tensor_tensor(out=y_[:], in0=ucx[:, 0:512], scalar=convw[:, fc:fc + 1], in1=y_[:], op0=ALU.mult, op1=ALU.add),
                    reads=["ucx", ky, "vecs"], writes=[ky])
                dve(lambda e, y_=y_, fc=fc, pbb=pbb: e.tensor_tensor(out=YB[:, fc, :], in0=y_[:], in1=pbb[:], op=ALU.mult), reads=[ky, pkb], writes=[("og", fc)])
                act(lambda e, fc=fc: e.activation(out=uc_carry[:, fc, :], in_=ucx[:, 512:514], func=AF.Copy), reads=["ucx"], writes=["uc_carry"])
            hook.start()
            outproj(4, KC, 512, YB, [("og", j) for j in range(KC)], mod(l, 2), t0, TT, src, stepper=hook)

        def pool_mixer(l, q_, t0, xsrc_ap, hook):
            TT = 512
            a_ap, s_ap = avec[:, (l * 2) * KC:(l * 2 + 1) * KC], mod(l, 0)
            MX = OG
            tbg = t0 // 512
            mdma("sync", invc[:], invc_d[:, t0:t0 + TT].partition_broadcast(128), writes=["invc"])
            n = 16 + TT
            for kp in range(0, KC, 2):
                gi = kp // 4
                ch = []
                for ci, kc in enumerate((kp, kp + 1)):
                    xb, xk, xsem = xst.next()
                    P.dma("sync", xb[:], xsrc_ap[:, kc, t0:t0 + TT], xsem, reads=[("x", kc, tbg)], writes=[xk])
                    hb, hkey = hx.next()
                    ch.append(dict(kc=kc, xb=xb, xk=xk, hb=hb, hkey=hkey, src=hb, skey=hkey, ci=ci))
                for c_ in ch:
                    dve(lambda e, xb=c_["xb"], kc=c_["kc"]: e.scalar_tensor_tensor(out=xb[:], in0=xb[:], scalar=a_ap[:, kc:kc + 1], in1=rv(tbg), op0=ALU.mult, op1=ALU.mult),
                        reads=[c_["xk"], "vecs", ("rvec", tbg)], writes=[c_["xk"]])
                for c_ in ch:
                    act(lambda e, hb=c_["hb"], kc=c_["kc"]: e.activation(out=hb[:, 0:16], in_=pool_carry[:, kc, :], func=AF.Copy), reads=["pool_carry"], writes=[c_["hkey"]])
                    act(lambda e, hb=c_["hb"], xb=c_["xb"], kc=c_["kc"]: e.activation(out=hb[:, 16:16 + TT], in_=xb[:], func=AF.Identity, bias=s_ap[:, kc:kc + 1], scale=1.0),
                        reads=[c_["xk"], "vecs"], writes=[c_["hkey"]])
                    act(lambda e, hb=c_["hb"], kc=c_["kc"]: e.activation(out=pool_carry[:, kc, :], in_=hb[:, TT:TT + 16], func=AF.Copy), reads=[c_["hkey"]], writes=["pool_carry"])
                for si in range(gi + 1):
                    m = 1 << si
                    for c_ in ch:
                        dst, dkey = Wp[c_["ci"] * 2 + si % 2], ("Wp", c_["ci"] * 2 + si % 2)
                        dve(lambda e, src=c_["src"], dst=dst, m=m: e.tensor_tensor(out=dst[:, m:n], in0=src[:, m:n], in1=src[:, 0:n - m], op=ALU.add), reads=[c_["skey"]], writes=[dkey])
                        c_["src"], c_["skey"] = dst, dkey
                for c_ in ch:
                    dve(lambda e, src=c_["src"], gi=gi: e.tensor_tensor(out=src[:, 16:n], in0=src[:, 16:n], in1=invc[:, gi, :], op=ALU.mult), reads=[c_["skey"], "invc"], writes=[c_["skey"]])
                for c_ in ch:
                    dve(lambda e, src=c_["src"], hb=c_["hb"], kc=c_["kc"]: e.tensor_tensor(out=MX[:, kc, :], in0=src[:, 16:n], in1=hb[:, 16:n], op=ALU.subtract),
                        reads=[c_["skey"], c_["hkey"]], writes=[("og", c_["kc"])])
            outproj(4, 4, 512, lambda b: (MX[:, b * 4:(b + 1) * 4, :], [("og", b * 4 + j) for j in range(4)]), None, g1p, t0, TT, xsrc_ap, fc_of=lambda b, f: b * 4 + f)

        def ffn(l, hf, hook):
            TT = 1024
            t0 = hf * TT
            if hf == 0:
                modst["cnt"], modst["b"] = 0, 0
            for b in range(22):
                slot, wk = w_get()
                wv = view(slot, KC, 512)
                for jj in range(2):
                    j = b * 2 + jj
                    for tb in range(2):
                        tsl = slice(tb * 512, (tb + 1) * 512)
                        hk = [("h", kc, tb) for kc in range(KC)]
                        pa, ka = psr.next()
                        pbb, kb = psr.next()
                        gemm_fm(pa[:], ka, wv, wk, slice(jj * 128, (jj + 1) * 128), A, hk, KC, tsl)
                        gemm_fm(pbb[:], kb, wv, wk, slice(256 + jj * 128, 256 + (jj + 1) * 128), A, hk, KC, tsl)
                        sb, sk = stf.next()
                        act(lambda e, sb=sb, pa=pa: e.activation(out=sb[:], in_=pa[:], func=AF.Silu), reads=[ka], writes=[sk])
                        dve(lambda e, sb=sb, pbb=pbb, j=j, tsl=tsl: e.tensor_tensor(out=G_ffn[:, j, tsl], in0=sb[:], in1=pbb[:], op=ALU.mult), reads=[sk, kb], writes=[("g", j)])
                w_done()
                maybe_mod(l)
            hook.start()
            outproj(16, JC, 128, G_ffn, [("g", j) for j in range(JC)], mod(l, 5), t0, TT, xs_v, after_block=lambda: maybe_mod(l), stepper=hook)

        slabs = []
        sub = 0
        for l in range(L):
            if sub >= nsub:
                break
            m = l % 4
            src = xin_v if l == 0 else xs_v
            for q_ in range(4):
                t0 = q_ * 512
                if m == 0:
                    slabs.append((lambda l=l, t0=t0, src=src: mixer_pro(l, t0, src), lambda hook, l=l, q_=q_, t0=t0, src=src: lin_attn(l, "ret", q_, t0, src, hook)))
                elif m == 1:
                    slabs.append((lambda l=l, t0=t0, src=src: mixer_pro(l, t0, src), lambda hook, l=l, q_=q_, t0=t0, src=src: conv_mixer(l, q_, t0, src, hook)))
                elif m == 2:
                    slabs.append((lambda l=l, t0=t0, src=src: mixer_pro(l, t0, src), lambda hook, l=l, q_=q_, t0=t0, src=src: lin_attn(l, "gla", q_, t0, src, hook)))
                else:
                    slabs.append((None, lambda hook, l=l, q_=q_, t0=t0, src=src: pool_mixer(l, q_, t0, src, hook)))
            sub += 1
            if sub >= nsub:
                break
            for hf in range(2):
                slabs.append((lambda l=l, hf=hf: ffn_pro(l, hf), lambda hook, l=l, hf=hf: ffn(l, hf, hook)))
            sub += 1
        class NoStep:
            def start(self):
                pass

            def step(self, n):
                pass

            def finish(self):
                pass

        cur = slabs[0][0]() if (slabs and slabs[0][0] is not None) else None
        if cur is not None:
            cur.finish()
        for i, (pro, body) in enumerate(slabs):
            nxt_f = slabs[i + 1][0] if i + 1 < len(slabs) else None
            st_ = nxt_f() if nxt_f is not None else NoStep()
            body(st_)
            st_.finish()
        assert wstate["next"] == len(plan), (wstate, len(plan))
        xfin = xs_v if slabs else xin_v

        out_toks = []
        osem = [P.sem("o0"), P.sem("o1"), P.sem("o2")]
        for tbg in range(4):
            ta = tbg * 512
            for kc in range(KC):
                xb, xk, xsem = xst.next()
                i = (xst.i - 1) % 3
                P.dma("sync", xb[:], xfin[:, kc, ta:ta + 512], xsem, reads=[("x", kc, tbg)], writes=[xk])
                if final:
                    dve(lambda e, xb=xb, kc=kc, tbg=tbg: e.scalar_tensor_tensor(out=xb[:], in0=xb[:], scalar=fing[:, kc:kc + 1], in1=rv(tbg),
                                                                              op0=ALU.mult, op1=ALU.mult), reads=[xk, "fing", ("rvec", tbg)], writes=[xk])
                out_toks.append(P.dma("sync", out_v[:, kc, ta:ta + 512], xb[:], osem[i], reads=[xk], writes=[("out", kc, ta)]))
        P.wait_all("sync", out_toks)
        P.run()
    return nc


def _lay(v, n):
    return np.ascontiguousarray(np.asarray(v, np.float32).reshape(n, 128).T)


def make_in_maps(inputs, cores):
    f = lambda k: np.asarray(inputs[k])
    shared = {
        "w_mod": np.ascontiguousarray(f("w_mod"), np.float32),
        "b_mod_l": np.ascontiguousarray(np.concatenate([_lay(f("b_mod")[l], 96) for l in range(L)], axis=1)),
        "n1g_l": np.ascontiguousarray(np.concatenate([_lay(f("norm1_g")[l], KC) for l in range(L)], axis=1)),
        "n2g_l": np.ascontiguousarray(np.concatenate([_lay(f("norm2_g")[l], KC) for l in range(L)], axis=1)),
        "fing_l": _lay(f("final_g"), KC),
        "ret_w_in": np.ascontiguousarray(f("ret_w_in")[0], np.float32),
        "ret_w_out": np.ascontiguousarray(f("ret_w_out")[0], np.float32),
        "conv_w_in": np.ascontiguousarray(f("conv_w_in")[0].reshape(D, 3, KC, 128).transpose(0, 2, 1, 3).reshape(D, 3 * D), np.float32),
        "conv_w_l": np.ascontiguousarray(np.concatenate([_lay(f("conv_w")[0][j], KC) for j in range(3)], axis=1)),
        "conv_w_out": np.ascontiguousarray(f("conv_w_out")[0], np.float32),
        "gla_w_in": np.ascontiguousarray(f("gla_w_in")[0], np.float32),
        "gla_w_up": np.ascontiguousarray(f("gla_w_gate_up")[0], np.float32),
        "gla_negb_l": np.ascontiguousarray(_lay(f("gla_b_gate")[0], 8) * np.float32(-1.0)),
        "gla_w_out": np.ascontiguousarray(f("gla_w_out")[0], np.float32),
        "pool_w": np.ascontiguousarray(f("pool_w")[0].reshape(D, 512), np.float32),
        "pool_sc_l": _lay(f("pool_scale")[0], KC),
        "ffn_w_in": np.ascontiguousarray(f("ffn_w_in"), np.float32),
        "ffn_w_out": np.ascontiguousarray(f("ffn_w_out"), np.float32),
        "inv_freq": np.power(np.float32(10000.0), -np.linspace(0.0, 1.0, 128, dtype=np.float32)).astype(np.float32).reshape(128, 1),
        "ident": np.eye(128, dtype=np.float32),
        "m01T": np.triu(np.ones((128, 128), np.float32)),
        "cpos": np.arange(1, 129, dtype=np.float32).reshape(1, 128),
        "invcnt": np.stack([1.0 / np.minimum(np.arange(1, S + 1), w) for w in (2, 4, 8, 16)]).astype(np.float32),
    }
    x, c, pos = f("x"), f("c"), f("positions")
    maps = []
    for b in cores:
        m = dict(shared)
        m["xT"] = np.ascontiguousarray(x[b].T, np.float32)
        m["c_l"] = _lay(c[b], KC)
        m["pos"] = np.ascontiguousarray(pos[b].reshape(1, S), np.int32)
        maps.append(m)
    return maps


_NC_CACHE = {}


def kernel(**inputs):
    n = 8
    if "full" not in _NC_CACHE:
        _NC_CACHE["full"] = build_program()
    nc = _NC_CACHE["full"]
    in_maps = make_in_maps(inputs, list(range(n)))
    res = run_bass_kernel_spmd(nc, in_maps, core_ids=list(range(n)))
    out = np.stack([np.ascontiguousarray(res.results[b]["outT"].T) for b in range(n)], axis=0)
    return out.astype(np.float32)
```
